# Optimizing a Trainium2 kernel written in Bass

```python
import jax
import jax.numpy as jnp
from jax import lax
import numpy as np

D_MODEL = 1024
BATCH = 8
SEQ = 4096
DEPTH = 4

GRID_W = 64
CTX_LEN = 256
N_HEADS = 4
D_INNER = 2 * D_MODEL
HEAD_V = D_INNER // N_HEADS
MLSTM_QK = D_MODEL
MLSTM_HEAD_QK = MLSTM_QK // N_HEADS
CONV_K = 3
GLA_KEY = D_MODEL // 2
GLA_HEAD_K = GLA_KEY // N_HEADS
GLA_RANK = 16
GLA_TAU = 16.0
RET_KEY = D_MODEL
RET_HEAD_K = RET_KEY // N_HEADS
ROPE_BASE = 10000.0
CHUNK = 64
NORM_EPS = 1e-6
NEG_INF = -1e30
MIXERS = ('mlstm', 'gla', 'retention')
MLSTM_IN = 2 * MLSTM_QK + 2 * D_INNER + 4 * N_HEADS
GLA_IN = 2 * GLA_KEY + 2 * D_INNER + 2 * GLA_RANK
RET_IN = 2 * RET_KEY + 2 * D_INNER

kernel_name = 'hybrid_mlstm_gla_retention_prefix_dit'


def rmsnorm(x, w):
    xf = x.astype(jnp.float32)
    y = xf * lax.rsqrt(jnp.mean(xf * xf, axis=-1, keepdims=True) + NORM_EPS)
    return (y * w.astype(jnp.float32)).astype(x.dtype)


def head_norm(y, w, center):
    if center:
        y = y - jnp.mean(y, axis=-1, keepdims=True)
    y = y * lax.rsqrt(jnp.mean(y * y, axis=-1, keepdims=True) + NORM_EPS)
    return y.reshape(y.shape[:2] + (-1,)) * w.astype(jnp.float32)


def flip_parts(t, n_ctx):
    return jnp.concatenate([jnp.flip(t[:, :n_ctx], axis=1), jnp.flip(t[:, n_ctx:], axis=1)], axis=1)


def to_chunks(t):
    b, l = t.shape[:2]
    return jnp.moveaxis(t.reshape((b, l // CHUNK, CHUNK) + t.shape[2:]), 1, 0)


def from_chunks(t):
    t = jnp.moveaxis(t, 0, 1)
    return t.reshape((t.shape[0], -1) + t.shape[3:])


def mlstm_scan(q, k, v, log_i, log_f):
    b_, _, h_, dk = q.shape
    dv = v.shape[-1]
    tri = jnp.tril(jnp.ones((CHUNK, CHUNK), dtype=bool))

    def step(carry, inp):
        c_st, n_st, m_st = carry
        qc, kc, vc, ic, fc = inp
        ic = ic.transpose(0, 2, 1)
        b = jnp.cumsum(fc.transpose(0, 2, 1), axis=-1)
        d = jnp.where(tri, b[..., :, None] - b[..., None, :] + ic[..., None, :], NEG_INF)
        inter = b + m_st[..., None]
        m_row = jnp.maximum(inter, jnp.max(d, axis=-1))
        s = jnp.einsum('bjhd,blhd->bhjl', qc, kc) * jnp.exp(d - m_row[..., None])
        w_inter = jnp.exp(inter - m_row)
        numer = (jnp.einsum('bhjl,blhv->bjhv', s, vc)
                 + jnp.einsum('bjhd,bhdv->bjhv', qc, c_st) * w_inter.transpose(0, 2, 1)[..., None])
        denom = jnp.sum(s, axis=-1) + jnp.einsum('bjhd,bhd->bhj', qc, n_st) * w_inter
        floor = jnp.maximum(jnp.abs(denom), jnp.exp(-m_row))
        h = numer / floor.transpose(0, 2, 1)[..., None]
        b_end = b[..., -1]
        g = b_end[..., None] - b + ic
        m_new = jnp.maximum(b_end + m_st, jnp.max(g, axis=-1))
        decay = jnp.exp(b_end + m_st - m_new)
        wk = kc * jnp.exp(g - m_new[..., None]).transpose(0, 2, 1)[..., None]
        c_new = decay[..., None, None] * c_st + jnp.einsum('blhd,blhv->bhdv', wk, vc)
        n_new = decay[..., None] * n_st + jnp.sum(wk, axis=1)
        return (c_new, n_new, m_new), h

    init = (jnp.zeros((b_, h_, dk, dv), jnp.float32),
            jnp.zeros((b_, h_, dk), jnp.float32),
            jnp.zeros((b_, h_), jnp.float32))
    _, hs = lax.scan(step, init, (to_chunks(q), to_chunks(k), to_chunks(v), to_chunks(log_i), to_chunks(log_f)))
    return from_chunks(hs)


def decay_scan(q, k, v, log_a):
    b_, _, h_, dk = q.shape
    dv = v.shape[-1]
    per_channel = log_a.shape[-1] > 1
    tri = jnp.tril(jnp.ones((CHUNK, CHUNK), dtype=bool))

    def step(s_st, inp):
        qc, kc, vc, ac = inp
        b = jnp.cumsum(ac, axis=1)
        if per_channel:
            rel = jnp.where(tri[None, :, :, None, None], b[:, :, None] - b[:, None], NEG_INF)
            s = jnp.einsum('bjhd,blhd,bjlhd->bhjl', qc, kc, jnp.exp(rel))
        else:
            rel = jnp.where(tri[None, :, :, None], b[:, :, None, :, 0] - b[:, None, :, :, 0], NEG_INF)
            s = jnp.einsum('bjhd,blhd->bhjl', qc, kc) * jnp.exp(rel).transpose(0, 3, 1, 2)
        o = jnp.einsum('bhjl,blhv->bjhv', s, vc) + jnp.einsum('bjhd,bhdv->bjhv', qc * jnp.exp(b), s_st)
        b_end = b[:, -1:]
        s_new = (jnp.exp(b_end[:, 0])[..., None] * s_st
                 + jnp.einsum('blhd,blhv->bhdv', kc * jnp.exp(b_end - b), vc))
        return s_new, o

    init = jnp.zeros((b_, h_, dk, dv), jnp.float32)
    _, os_ = lax.scan(step, init, (to_chunks(q), to_chunks(k), to_chunks(v), to_chunks(log_a)))
    return from_chunks(os_)


def bidirectional(scan_fn, n_ctx, shared, fw, bw):
    out_fw = scan_fn(*shared, *fw)
    out_bw = scan_fn(*[flip_parts(t, n_ctx) for t in shared], *[flip_parts(t, n_ctx) for t in bw])
    return out_fw + flip_parts(out_bw, n_ctx)


def axial_rope(t):
    s, dh = t.shape[1], t.shape[-1]
    quarter = dh // 4
    pos = jnp.arange(s, dtype=jnp.int32)
    row = (pos // GRID_W).astype(jnp.float32)
    col = (pos % GRID_W).astype(jnp.float32)
    inv_freq = ROPE_BASE ** (-jnp.arange(quarter, dtype=jnp.float32) / quarter)

    def rot(u, p):
        ang = p[:, None] * inv_freq[None, :]
        cos = jnp.cos(ang)[None, :, None, :]
        sin = jnp.sin(ang)[None, :, None, :]
        u1, u2 = u[..., :quarter], u[..., quarter:]
        return jnp.concatenate([u1 * cos - u2 * sin, u1 * sin + u2 * cos], axis=-1)

    return jnp.concatenate([rot(t[..., :dh // 2], row), rot(t[..., dh // 2:], col)], axis=-1)


def grid_conv(u_ctx, u_lat, conv_w, conv_b):
    b, s, ch = u_lat.shape
    n_ctx = u_ctx.shape[1]
    rows = s // GRID_W
    lat = lax.conv_general_dilated(u_lat.reshape(b, rows, GRID_W, ch), conv_w[:, :, None, :],
                                   window_strides=(1, 1), padding='SAME',
                                   dimension_numbers=('NHWC', 'HWIO', 'NHWC'),
                                   feature_group_count=ch).reshape(b, s, ch)
    w1 = conv_w[CONV_K // 2]
    pad = jnp.pad(u_ctx, ((0, 0), (CONV_K // 2, CONV_K // 2), (0, 0)))
    ctx_out = sum(pad[:, i:i + n_ctx] * w1[i] for i in range(CONV_K))
    return jnp.concatenate([ctx_out, lat], axis=1) + conv_b


def mlstm_mixer(u, n_ctx, in_w, conv_w, conv_b, gate_b, head_norm_w):
    b, l, _ = u.shape
    p = u @ in_w
    qk, v, z, gates = jnp.split(p, [2 * MLSTM_QK, 2 * MLSTM_QK + D_INNER, 2 * MLSTM_QK + 2 * D_INNER], axis=-1)
    qk = jax.nn.silu(grid_conv(qk[:, :n_ctx], qk[:, n_ctx:], conv_w, conv_b))
    q, k = jnp.split(qk, 2, axis=-1)
    q = q.reshape(b, l, N_HEADS, MLSTM_HEAD_QK).astype(jnp.float32)
    k = k.reshape(b, l, N_HEADS, MLSTM_HEAD_QK).astype(jnp.float32) * (MLSTM_HEAD_QK ** -0.5)
    v = v.reshape(b, l, N_HEADS, HEAD_V).astype(jnp.float32)
    g = gates.astype(jnp.float32) + gate_b.astype(jnp.float32)
    i_fw, f_fw, i_bw, f_bw = jnp.split(g, 4, axis=-1)
    h = bidirectional(mlstm_scan, n_ctx, (q, k, v),
                      (i_fw, jax.nn.log_sigmoid(f_fw)), (i_bw, jax.nn.log_sigmoid(f_bw)))
    h = head_norm(h, head_norm_w, center=True)
    return (h * jax.nn.silu(z.astype(jnp.float32))).astype(u.dtype)


def gla_mixer(u, n_ctx, in_w, gk_w2, gk_b, head_norm_w):
    b, l, _ = u.shape
    p = u @ in_w
    q, k, v, z, lr = jnp.split(p, [GLA_KEY, 2 * GLA_KEY, 2 * GLA_KEY + D_INNER, 2 * GLA_KEY + 2 * D_INNER], axis=-1)
    q = q.reshape(b, l, N_HEADS, GLA_HEAD_K).astype(jnp.float32) * (GLA_HEAD_K ** -0.5)
    k = k.reshape(b, l, N_HEADS, GLA_HEAD_K).astype(jnp.float32)
    v = v.reshape(b, l, N_HEADS, HEAD_V).astype(jnp.float32)
    lr_fw, lr_bw = jnp.split(lr, 2, axis=-1)

    def log_alpha(code, w2, bias):
        pre = jnp.einsum('blr,rk->blk', code, w2).astype(jnp.float32) + bias.astype(jnp.float32)
        return (jax.nn.log_sigmoid(pre) / GLA_TAU).reshape(b, l, N_HEADS, GLA_HEAD_K)

    o = bidirectional(decay_scan, n_ctx, (q, k, v),
                      (log_alpha(lr_fw, gk_w2[0], gk_b[0]),), (log_alpha(lr_bw, gk_w2[1], gk_b[1]),))
    o = head_norm(o, head_norm_w, center=False)
    return (o * jax.nn.silu(z.astype(jnp.float32))).astype(u.dtype)


def retention_mixer(u, n_ctx, in_w, decay_logit, head_norm_w):
    b, l, _ = u.shape
    p = u @ in_w
    q, k, v, z = jnp.split(p, [RET_KEY, 2 * RET_KEY, 2 * RET_KEY + D_INNER], axis=-1)
    q = q.reshape(b, l, N_HEADS, RET_HEAD_K).astype(jnp.float32)
    k = k.reshape(b, l, N_HEADS, RET_HEAD_K).astype(jnp.float32)
    q = jnp.concatenate([q[:, :n_ctx], axial_rope(q[:, n_ctx:])], axis=1)
    k = jnp.concatenate([k[:, :n_ctx], axial_rope(k[:, n_ctx:])], axis=1) * (RET_HEAD_K ** -0.5)
    v = v.reshape(b, l, N_HEADS, HEAD_V).astype(jnp.float32)
    log_gamma = jax.nn.log_sigmoid(decay_logit.astype(jnp.float32))
    la_fw = jnp.broadcast_to(log_gamma[0][None, None, :, None], (b, l, N_HEADS, 1))
    la_bw = jnp.broadcast_to(log_gamma[1][None, None, :, None], (b, l, N_HEADS, 1))
    o = bidirectional(decay_scan, n_ctx, (q, k, v), (la_fw,), (la_bw,))
    o = head_norm(o, head_norm_w, center=True)
    return (o * jax.nn.silu(z.astype(jnp.float32))).astype(u.dtype)


def hybrid_layer(kind, ctx_h, lat_h, c, c_ctx, norm_w, ada_w, ada_b, out_w, mixer_params, last):
    n_ctx = ctx_h.shape[1]
    ada_lat = jax.nn.silu(c) @ ada_w + ada_b
    ada_ctx = jax.nn.silu(c_ctx) @ ada_w + ada_b
    sh_l, sc_l, g_l = jnp.split(ada_lat[:, None, :], 3, axis=-1)
    sh_c, sc_c, g_c = jnp.split(ada_ctx, 3, axis=-1)
    u = jnp.concatenate([rmsnorm(ctx_h, norm_w) * (1 + sc_c) + sh_c,
                         rmsnorm(lat_h, norm_w) * (1 + sc_l) + sh_l], axis=1)
    if kind == 'mlstm':
        y = mlstm_mixer(u, n_ctx, *mixer_params)
    elif kind == 'gla':
        y = gla_mixer(u, n_ctx, *mixer_params)
    else:
        y = retention_mixer(u, n_ctx, *mixer_params)
    lat_h = lat_h + g_l * (y[:, n_ctx:] @ out_w)
    if not last:
        ctx_h = ctx_h + g_c * (y[:, :n_ctx] @ out_w)
    return ctx_h, lat_h


def setup_inputs(seed: int = 0) -> dict:
    key = jax.random.key(seed)
    keys = iter(jax.random.split(key, 64))

    def rnd(shape, std):
        return std * jax.random.normal(next(keys), shape, jnp.float32)

    inputs = {
        'x': rnd((BATCH, SEQ, D_MODEL), 1.0),
        'c': rnd((BATCH, D_MODEL), 1.0),
        'ctx': rnd((BATCH, CTX_LEN, D_MODEL), 1.0),
        'c_ctx': rnd((D_MODEL,), 1.0),
    }
    for i in range(DEPTH):
        kind = MIXERS[i % len(MIXERS)]
        p = 'l%d_' % i
        inputs[p + 'norm_w'] = 1.0 + rnd((D_MODEL,), 0.1)
        inputs[p + 'ada_w'] = rnd((D_MODEL, 3 * D_MODEL), 0.5 * D_MODEL ** -0.5)
        inputs[p + 'ada_b'] = rnd((3 * D_MODEL,), 0.02)
        if kind == 'mlstm':
            inputs[p + 'in_w'] = rnd((D_MODEL, MLSTM_IN), D_MODEL ** -0.5)
            inputs[p + 'conv_w'] = rnd((CONV_K, CONV_K, 2 * MLSTM_QK), 1.0 / CONV_K)
            inputs[p + 'conv_b'] = rnd((2 * MLSTM_QK,), 0.02)
            f_bias = jnp.linspace(3.0, 6.0, N_HEADS, dtype=jnp.float32)
            inputs[p + 'gate_b'] = jnp.concatenate([rnd((N_HEADS,), 0.1), f_bias + rnd((N_HEADS,), 0.1),
                                                    rnd((N_HEADS,), 0.1), f_bias + rnd((N_HEADS,), 0.1)])
        elif kind == 'gla':
            inputs[p + 'in_w'] = rnd((D_MODEL, GLA_IN), D_MODEL ** -0.5)
            inputs[p + 'gk_w2'] = rnd((2, GLA_RANK, GLA_KEY), GLA_RANK ** -0.5)
            inputs[p + 'gk_b'] = rnd((2, GLA_KEY), 0.5)
        else:
            inputs[p + 'in_w'] = rnd((D_MODEL, RET_IN), D_MODEL ** -0.5)
            p_decay = 2.0 ** (-5.0 - jnp.arange(N_HEADS, dtype=jnp.float32))
            inputs[p + 'decay_logit'] = (jnp.log1p(-p_decay) - jnp.log(p_decay))[None, :] + rnd((2, N_HEADS), 0.1)
        inputs[p + 'head_norm_w'] = 1.0 + rnd((D_INNER,), 0.1)
        inputs[p + 'out_w'] = rnd((D_INNER, D_MODEL), D_INNER ** -0.5)
    inputs['final_norm_w'] = 1.0 + rnd((D_MODEL,), 0.1)
    return inputs


def reference(x, c, ctx, c_ctx,
              l0_norm_w, l0_ada_w, l0_ada_b, l0_in_w, l0_conv_w, l0_conv_b, l0_gate_b, l0_head_norm_w, l0_out_w,
              l1_norm_w, l1_ada_w, l1_ada_b, l1_in_w, l1_gk_w2, l1_gk_b, l1_head_norm_w, l1_out_w,
              l2_norm_w, l2_ada_w, l2_ada_b, l2_in_w, l2_decay_logit, l2_head_norm_w, l2_out_w,
              l3_norm_w, l3_ada_w, l3_ada_b, l3_in_w, l3_conv_w, l3_conv_b, l3_gate_b, l3_head_norm_w, l3_out_w,
              final_norm_w):
    layers = (
        ('mlstm', l0_norm_w, l0_ada_w, l0_ada_b, l0_out_w,
         (l0_in_w, l0_conv_w, l0_conv_b, l0_gate_b, l0_head_norm_w)),
        ('gla', l1_norm_w, l1_ada_w, l1_ada_b, l1_out_w,
         (l1_in_w, l1_gk_w2, l1_gk_b, l1_head_norm_w)),
        ('retention', l2_norm_w, l2_ada_w, l2_ada_b, l2_out_w,
         (l2_in_w, l2_decay_logit, l2_head_norm_w)),
        ('mlstm', l3_norm_w, l3_ada_w, l3_ada_b, l3_out_w,
         (l3_in_w, l3_conv_w, l3_conv_b, l3_gate_b, l3_head_norm_w)),
    )
    ctx_h, lat_h = ctx, x
    for i in range(DEPTH):
        kind, norm_w, ada_w, ada_b, out_w, mixer_params = layers[i]
        ctx_h, lat_h = hybrid_layer(kind, ctx_h, lat_h, c, c_ctx, norm_w, ada_w, ada_b, out_w,
                                    mixer_params, last=(i == DEPTH - 1))
    return rmsnorm(lat_h, final_norm_w)
```

```python
import math
import os
import contextlib
import numpy as np
import ml_dtypes
import concourse.bass as bass
import concourse.mybir as mybir
from concourse.bass_utils import run_bass_kernel_spmd

F32 = mybir.dt.float32
BF16 = mybir.dt.bfloat16
AF = mybir.ActivationFunctionType
ALU = mybir.AluOpType

LTOK = 4352
NT = 34
NCTX = 256
D = 1024
DI = 2048
EPS = 1e-6
KINDS = ("mlstm", "gla", "ret", "mlstm")
NIN = {"mlstm": 6160, "gla": 5152, "ret": 6144}
ARENA_WORDS = 51200
STRICT_WAR = os.environ.get("STRICT_WAR", "1") == "1"


class Res:
    __slots__ = ("name", "writer", "readers", "excl")

    def __init__(self, name="", excl=False):
        self.name = name
        self.writer = None
        self.readers = []
        self.excl = excl


class Op:
    __slots__ = ("eng", "fn", "waits", "signal", "token", "is_dma", "idx", "drain")

    def __init__(self, eng, fn, is_dma):
        self.eng = eng
        self.fn = fn
        self.waits = []
        self.signal = False
        self.token = None
        self.is_dma = is_dma
        self.idx = -1
        self.drain = None


class Sched:
    ENGS = ("pe", "act", "dve", "pool", "sp")
    DMAQ = ("sp", "pool", "act")
    NDMA_SEM = 20

    def __init__(self, nc):
        self.nc = nc
        self.ops = {e: [] for e in self.ENGS}
        self.nops = 0
        self.bar_scr = None
        self.bar_ps = None
        self.bar_res = None
        self.scr_res = None

    def _tiny(self, engname, e):
        scr = self.bar_scr
        if engname == "pe":
            return e.matmul(self.bar_ps[0:1, 0:1], lhsT=scr[0:1, 8:9], rhs=scr[0:1, 9:10], start=True, stop=True)
        if engname == "act":
            return e.activation(out=scr[0:1, 0:1], in_=scr[0:1, 10:11], func=AF.Copy)
        if engname == "dve":
            return e.memset(scr[0:1, 2:3], 0.0)
        if engname == "pool":
            return e.memset(scr[0:1, 4:5], 0.0)
        return e.dma_start(out=scr[0:1, 6:7], in_=scr[0:1, 11:12])

    def _deps(self, op, reads, writes):
        ex = [r for r in reads if r.excl and r not in writes]
        if ex:
            writes = list(writes) + ex
        deps = []
        for r in reads:
            if r.writer is not None:
                deps.append((r.writer, "raw"))
        for w in writes:
            if w.writer is not None:
                deps.append((w.writer, "waw"))
            for rd in w.readers:
                deps.append((rd, "war"))
        seen = set()
        for p, kind in deps:
            if p is op or id(p) in seen:
                continue
            if p.eng == op.eng and not p.is_dma and not op.is_dma:
                if op.eng == "pe":
                    continue
                if kind == "war" and not STRICT_WAR:
                    continue
            seen.add(id(p))
            op.waits.append(p)
            p.signal = True
        for r in reads:
            r.readers.append(op)
        for w in writes:
            w.writer = op
            w.readers = []

    def op(self, eng, fn, reads=(), writes=()):
        o = Op(eng, fn, False)
        o.idx = self.nops
        self.nops += 1
        self._deps(o, reads, writes)
        self.ops[eng].append(o)
        return o

    def dma(self, eng, out, in_, reads=(), writes=(), **kw):
        o = Op(eng, lambda e: e.dma_start(out=out, in_=in_, **kw), True)
        o.idx = self.nops
        self.nops += 1
        self._deps(o, reads, writes)
        self.ops[eng].append(o)
        return o

    def barrier(self):
        pre = []
        for e in self.ENGS:
            o = Op(e, ("bar_pre", e), False)
            o.idx = self.nops
            self.nops += 1
            if not hasattr(self, "tiny_res"):
                self.tiny_res = {x: Res("tiny_" + x) for x in self.ENGS}
            self._deps(o, [self.scr_res] if self.scr_res is not None else [],
                       [self.tiny_res[e]] + ([self.bar_res] if (e == "pe" and self.bar_res is not None) else []))
            o.signal = True
            self.ops[e].append(o)
            pre.append(o)
        for e in self.ENGS:
            o = Op(e, ("bar_post", e), False)
            o.idx = self.nops
            self.nops += 1
            o.waits = [p for p in pre if p.eng != e]
            self.ops[e].append(o)

    def emit(self, final_ops=()):
        nc = self.nc
        with contextlib.ExitStack() as es:
            esem = {e: es.enter_context(nc.semaphore("s_" + e)) for e in self.ENGS}
            dsem = {e: [es.enter_context(nc.semaphore("d_%s_%d" % (e, i))) for i in range(self.NDMA_SEM)]
                    for e in self.DMAQ}
            block = es.enter_context(nc.Block())
            ecount = {e: 0 for e in self.ENGS}
            dcount = {e: [0] * self.NDMA_SEM for e in dsem}
            dnext = {e: 0 for e in dsem}
            allops = sorted([o for e in self.ENGS for o in self.ops[e]], key=lambda o: o.idx)
            pre_wait = {}
            for o in allops:
                if o.is_dma:
                    k = dnext[o.eng] % self.NDMA_SEM
                    dnext[o.eng] += 1
                    if dcount[o.eng][k] > 0:
                        pre_wait[id(o)] = (dsem[o.eng][k], 16 * dcount[o.eng][k])
                    dcount[o.eng][k] += 1
                    o.token = (dsem[o.eng][k], 16 * dcount[o.eng][k])
                elif o.signal:
                    ecount[o.eng] += (16 if o.eng == "sp" else 1)
                    o.token = (esem[o.eng], ecount[o.eng])
                    if isinstance(o.fn, tuple) and o.eng in dsem:
                        o.drain = [(dsem[o.eng][k2], 16 * dcount[o.eng][k2])
                                   for k2 in range(self.NDMA_SEM) if dcount[o.eng][k2] > 0]
            final_tokens = [o.token for o in final_ops]

            def run(engname, e):
                waited = {}

                def wait(sem, val):
                    if waited.get(id(sem), 0) >= val:
                        return
                    waited[id(sem)] = val
                    e.wait_ge(sem, val)

                for o in self.ops[engname]:
                    for p in o.waits:
                        wait(*p.token)
                    if id(o) in pre_wait:
                        wait(*pre_wait[id(o)])
                    if isinstance(o.fn, tuple):
                        if o.fn[0] == "bar_pre":
                            for sem2, val2 in (o.drain or ()):
                                wait(sem2, val2)
                            ins = self._tiny(engname, e)
                            ins.then_inc(o.token[0], 16 if engname == "sp" else 1)
                        continue
                    ins = o.fn(e)
                    if o.is_dma:
                        ins.then_inc(o.token[0], 16)
                    elif o.signal:
                        ins.then_inc(o.token[0], 1)
                if engname == "sp":
                    for sem, val in final_tokens:
                        wait(sem, val)

            @block.tensor
            def _(e):
                run("pe", e)

            @block.scalar
            def _(e):
                run("act", e)

            @block.vector
            def _(e):
                run("dve", e)

            @block.gpsimd
            def _(e):
                run("pool", e)

            @block.sync
            def _(e):
                run("sp", e)


class Arena:
    def __init__(self, ap):
        self.ap = ap
        self.off = 0

    def reset(self):
        self.off = 0

    def f32(self, n, name=""):
        a = self.ap[:, self.off:self.off + n]
        self.off += n
        assert self.off <= ARENA_WORDS, ("arena overflow", name, self.off)
        return a, Res(name)

    def bf16(self, n, name=""):
        w = (n + 1) // 2
        a = self.ap[:, self.off:self.off + w].bitcast(BF16)[:, 0:n]
        self.off += w
        assert self.off <= ARENA_WORDS, ("arena overflow", name, self.off)
        return a, Res(name)


def ACT(out, in_, func, scale=None, bias=None, accum=None):
    kw = {}
    if scale is not None:
        kw["scale"] = scale
    if bias is not None:
        kw["bias"] = bias
    if accum is not None:
        kw["accum_out"] = accum
    return lambda e: e.activation(out=out, in_=in_, func=func, **kw)


def TT(out, in0, in1, op):
    return lambda e: e.tensor_tensor(out=out, in0=in0, in1=in1, op=op)


def TS(out, in0, s1, op0, s2=None, op1=None):
    if op1 is None:
        return lambda e: e.tensor_scalar(out=out, in0=in0, scalar1=s1, scalar2=None, op0=op0)
    return lambda e: e.tensor_scalar(out=out, in0=in0, scalar1=s1, scalar2=s2, op0=op0, op1=op1)


def STT(out, in0, scalar, in1, op0, op1):
    return lambda e: e.scalar_tensor_tensor(out=out, in0=in0, scalar=scalar, in1=in1, op0=op0, op1=op1)


def CP(out, in_):
    return lambda e: e.tensor_copy(out=out, in_=in_)


def MM(out, lhsT, rhs, start, stop):
    return lambda e: e.matmul(out, lhsT=lhsT, rhs=rhs, start=start, stop=stop)


def TR(out, in_, ident):
    return lambda e: e.transpose(out, in_, ident)


def MSET(ap, v):
    return lambda e: e.memset(ap, v)


def build(nlayers=4):
    nc = bass.Bass("TRN2", target_bir_lowering=False)

    def din(name, shape, dt=F32):
        return nc.dram_tensor(name, list(shape), dt, kind="ExternalInput").ap()

    def dscr(name, shape, dt=F32):
        return nc.dram_tensor(name, list(shape), dt, kind="Internal").ap()

    x_d = din("x", [4096, D])
    ctx_d = din("ctx", [NCTX, D])
    ccT_d = din("ccT", [128, 16])
    ident_d = din("ident16", [128, 128], BF16)
    maskF_d = din("maskF", [128, 128])
    maskB_d = din("maskB", [128, 128])
    cos_d = din("rope_cos", [128, 32 * 128])
    sin_d = din("rope_sin", [128, 32 * 128])
    fnw_d = din("final_norm_w", [D])
    W = []
    for li in range(4):
        kind = KINDS[li]
        p = "l%d_" % li
        w = dict(norm_w=din(p + "norm_w", [D]), ada_w=din(p + "ada_w", [D, 3 * D]), ada_b=din(p + "ada_b", [3 * D]),
                 in_w=din(p + "in_w", [D, NIN[kind]]), hnwT=din(p + "hnwT", [128, 16]), out_w=din(p + "out_w", [DI, D]))
        if kind == "mlstm":
            w["convw"] = din(p + "convw", [128, 16 * 9])
            w["convb"] = din(p + "convb", [128, 16])
            w["gate_b"] = din(p + "gate_b", [16])
        elif kind == "gla":
            w["w2"] = din(p + "w2", [16, 2 * 512])
            w["gkb"] = din(p + "gkb", [128, 8])
        else:
            w["decay"] = din(p + "decay", [8])
        W.append(w)
    out_d = nc.dram_tensor("out", [4096, D], F32, kind="ExternalOutput").ap()

    h_d = dscr("h_s", [LTOK, D])
    ada_d = dscr("ada_s", [4, 2, 3 * D])
    qT_d = [dscr("qT_s%d" % i, [1024, LTOK], BF16) for i in range(2)]
    kT_d = [dscr("kT_s%d" % i, [1024, LTOK], BF16) for i in range(2)]
    ktm_d = [dscr("ktm_s%d" % i, [LTOK, 1024], BF16) for i in range(2)]
    v_d = dscr("v_s", [LTOK, DI], BF16)
    z_d = dscr("z_s", [LTOK, DI], BF16)
    gates_d = dscr("gates_s", [LTOK, 16])
    ofw_d = dscr("ofw_s", [LTOK, DI])
    qkraw_d = dscr("qkraw_s", [1024, LTOK])
    code_d = dscr("code_s", [32, LTOK])

    with contextlib.ExitStack() as es:
        def sb(name, shape, dt=F32):
            return es.enter_context(nc.sbuf_tensor("sb_" + name, list(shape), dt))

        arena_t = sb("arena", [128, ARENA_WORDS])
        bscr = sb("bscr", [1, 16], BF16)
        ident = sb("ident", [128, 128], BF16)
        maskF = sb("maskF", [128, 128])
        maskB = sb("maskB", [128, 128])
        onesF = sb("onesF", [128, 128])
        ones16 = sb("ones16", [128, 2], BF16)
        ccol = sb("ccol", [128, 8])
        gdec = sb("gdec", [128, 2 * 4 * NT])
        ps_all = es.enter_context(nc.psum_tensor("ps_all", [128, 4096], F32))
        PB = [ps_all[:, i * 512:(i + 1) * 512] for i in range(8)]
        PBR = [Res("pb%d" % i, excl=True) for i in range(8)]
        PB16 = [PB[i].bitcast(BF16) for i in range(8)]

        S = Sched(nc)
        S.bar_scr = bscr
        S.bar_ps = PB[7]
        S.bar_res = PBR[7]
        S.scr_res = Res("bscr")
        A = Arena(arena_t[:])
        Rc = Res("consts")

        S.op("dve", MSET(bscr[:], 0.0), writes=[S.scr_res])
        S.dma("sp", ident[:], ident_d[:, :], writes=[Rc])
        S.dma("sp", maskF[:], maskF_d[:, :], writes=[Rc])
        S.dma("sp", maskB[:], maskB_d[:, :], writes=[Rc])
        S.op("pool", MSET(onesF[:], 1.0), writes=[Rc])
        S.op("pool", MSET(ones16[:], 1.0), writes=[Rc])
        S.op("pool", MSET(ccol[:, 0:1], EPS), writes=[Rc])
        S.op("pool", MSET(ccol[:, 1:2], 1.0), writes=[Rc])
        S.op("pool", MSET(ccol[:, 2:3], -math.log(16.0)), writes=[Rc])
        S.op("pool", MSET(ccol[:, 3:4], math.log(128.0 ** -0.5)), writes=[Rc])
        eps_c, one_c, nl16_c, lqs_c = ccol[:, 0:1], ccol[:, 1:2], ccol[:, 2:3], ccol[:, 3:4]

        A.reset()
        ccT, Rcc = A.f32(16, "ccT")
        scT, Rsc = A.f32(16, "scT")
        adab, Radab = A.f32(3 * D, "adab")
        adasb, Radasb = A.f32(3 * D, "adasb")
        wst2 = [A.f32(8 * 512, "wst%d" % i) for i in range(2)]
        S.dma("sp", ccT, ccT_d[:, :], writes=[Rcc])
        S.op("act", ACT(scT, ccT, AF.Silu), reads=[Rcc], writes=[Rsc])
        for li in range(nlayers):
            S.dma("sp", adab[0:2, :], W[li]["ada_b"].partition_broadcast(2), writes=[Radab])
            aw = W[li]["ada_w"].rearrange("(t p) n -> p t n", p=128)
            for nb in range(6):
                wst, Rw = wst2[nb % 2]
                wst3 = wst.rearrange("p (t n) -> p t n", n=512)
                S.dma("sp", wst3, aw[:, :, nb * 512:(nb + 1) * 512], writes=[Rw])
                pb = nb % 2
                for kt in range(8):
                    S.op("pe", MM(PB[pb][0:2, :], scT[:, 2 * kt:2 * kt + 2], wst3[:, kt, :], kt == 0, kt == 7),
                         reads=[Rsc, Rw], writes=[PBR[pb]])
                S.op("dve", TT(adasb[0:2, nb * 512:(nb + 1) * 512], PB[pb][0:2, :], adab[0:2, nb * 512:(nb + 1) * 512], ALU.add),
                     reads=[PBR[pb], Radab], writes=[Radasb])
            S.dma("sp", ada_d[li], adasb[0:2, :], reads=[Radasb])
        S.barrier()

        final_ops = []
        for li in range(nlayers):
            kind = KINDS[li]
            Wl = W[li]
            last = (li == nlayers - 1)

            def h_src(tt, li=li):
                if li > 0:
                    return h_d[tt * 128:(tt + 1) * 128, :]
                if tt < 2:
                    return ctx_d[tt * 128:(tt + 1) * 128, :]
                return x_d[(tt - 2) * 128:(tt - 1) * 128, :]
            scalar_mix = kind in ("mlstm", "ret")
            A.reset()
            uT, RuT = A.bf16(8 * LTOK, "uT")
            uT3 = uT.rearrange("p (k t) -> p k t", t=LTOK)
            hin = [A.f32(D, "hin%d" % i) for i in range(2)]
            utm = [A.bf16(D, "utm%d" % i) for i in range(2)]
            tmp32, Rtmp = A.f32(D, "tmp32")
            sq, Rsq = A.f32(D, "sq")
            nwbc, Rnw = A.f32(D, "nwbc")
            scbc, Rscb = A.f32(D, "scbc")
            shbc, Rshb = A.f32(D, "shbc")
            Abc, RA = A.f32(D, "Abc")
            ss, Rss = A.f32(NT, "ss")
            rs, Rrs = A.f32(NT, "rs")

            S.dma("sp", nwbc, Wl["norm_w"].partition_broadcast(128), writes=[Rnw])
            Rsst = [Res() for _ in range(NT)]

            def p1_stats(tt):
                hb, Rh = hin[tt % 2]
                S.dma("sp", hb, h_src(tt), writes=[Rh])
                S.op("act", ACT(sq, hb, AF.Square, accum=ss[:, tt:tt + 1]), reads=[Rh], writes=[Rsq, Rsst[tt]])
                S.op("act", ACT(rs[:, tt:tt + 1], ss[:, tt:tt + 1], AF.Ln, scale=1.0 / D, bias=eps_c), reads=[Rsst[tt], Rc], writes=[Rsst[tt]])
                S.op("act", ACT(rs[:, tt:tt + 1], rs[:, tt:tt + 1], AF.Exp, scale=-0.5), reads=[Rsst[tt]], writes=[Rsst[tt]])

            for tt in range(NT):
                if tt == 0 or tt == 2:
                    which = 1 if tt == 0 else 0
                    S.dma("sp", scbc, ada_d[li, which, D:2 * D].partition_broadcast(128), writes=[Rscb])
                    S.dma("sp", shbc, ada_d[li, which, 0:D].partition_broadcast(128), writes=[Rshb])
                    S.op("dve", STT(Abc, scbc, 1.0, nwbc, ALU.add, ALU.mult), reads=[Rscb, Rnw], writes=[RA])
                hb, Rh = hin[tt % 2]
                ub, Ru = utm[tt % 2]
                if tt == 0:
                    p1_stats(0)
                if tt + 1 < NT:
                    p1_stats(tt + 1)
                S.op("dve", STT(tmp32, hb, rs[:, tt:tt + 1], Abc, ALU.mult, ALU.mult), reads=[Rh, Rsst[tt], RA, Rsq], writes=[Rtmp])
                S.op("dve", TT(ub, tmp32, shbc, ALU.add), reads=[Rtmp, Rshb], writes=[Ru])
                pb = tt % 2
                for kt in range(8):
                    S.op("pe", TR(PB16[pb][:, kt * 128:(kt + 1) * 128], ub[:, kt * 128:(kt + 1) * 128], ident[:]),
                         reads=[Ru, Rc], writes=[PBR[pb]])
                S.op("act", CP_ACT(uT3[:, :, tt * 128:(tt + 1) * 128], PB16[pb].rearrange("p (k t) -> p k t", t=128)),
                     reads=[PBR[pb]], writes=[RuT])
            S.barrier()
            A.reset()
            uT, RuT = A.bf16(8 * LTOK, "uT")
            uT3 = uT.rearrange("p (k t) -> p k t", t=LTOK)
            wstd = [A.f32(8 * 512, "wst%d" % i) for i in range(2)]
            w16 = [A.bf16(8 * 512, "w16_%d" % i) for i in range(2)]
            ev = [A.bf16(512, "ev%d" % i) for i in range(3)]
            ev32 = [A.f32(16, "ev32_%d" % i) for i in range(2)]
            tstage = [A.bf16(8 * 128, "tstage%d" % i) for i in range(2)]
            wcount = [0]
            in_w3 = Wl["in_w"].rearrange("(t p) n -> p t n", p=128)

            if kind == "mlstm":
                wgroups = [(2048 + g * 512, 512) for g in range(8)] + [(6144, 16)] + [(g * 512, 512) for g in range(4)]
            elif kind == "ret":
                wgroups = [(2048 + g * 512, 512) for g in range(8)] + [(g * 512, 512) for g in range(4)]
            else:
                wgroups = [(1024 + g * 512, 512) for g in range(8)] + [(5120, 32)] + [(g * 512, 512) for g in range(2)]
            wloaded = {}

            def ensure_w(i):
                if i >= len(wgroups) or i in wloaded:
                    return
                c0, n = wgroups[i]
                wb, Rwb = w16[i % 2]
                wsb, Rwsb = wstd[i % 2]
                wb3 = wb.rearrange("p (t n) -> p t n", n=512)
                ws3 = wsb.rearrange("p (t n) -> p t n", n=512)
                S.dma("sp", ws3[:, :, 0:n], in_w3[:, :, c0:c0 + n], writes=[Rwsb])
                S.op("pool", CP(wb3[:, :, 0:n], ws3[:, :, 0:n]), reads=[Rwsb], writes=[Rwb])
                wloaded[i] = (wb3, Rwb)

            def load_w(c0, n):
                i = wcount[0]
                wcount[0] += 1
                assert wgroups[i] == (c0, n), (wgroups[i], c0, n)
                ensure_w(i)
                r = wloaded[i]
                if int(os.environ.get("WPREF", "1")):
                    ensure_w(i + 1)
                return r

            mrot = [0]

            def tm_group(c0, n, sink):
                wb3, Rwb = load_w(c0, n)
                for tt in range(NT):
                    pb = 2 + mrot[0] % 3
                    mrot[0] += 1
                    for kt in range(8):
                        S.op("pe", MM(PB[pb][:, 0:n], uT3[:, kt, tt * 128:(tt + 1) * 128], wb3[:, kt, 0:n], kt == 0, kt == 7),
                             reads=[RuT, Rwb], writes=[PBR[pb]])
                    sink(tt, PB[pb][:, 0:n], PBR[pb])

            evrot = [0]

            def sink_store(dst, dc0, n, dt_bf16=True):
                def f(tt, ps, Rps):
                    i = evrot[0] % (3 if dt_bf16 else 2)
                    evrot[0] += 1
                    eb, Re = ev[i] if dt_bf16 else ev32[i]
                    eng = "act" if (evrot[0] % 2) else "dve"
                    if eng == "act":
                        S.op("act", ACT(eb[:, 0:n], ps, AF.Copy), reads=[Rps], writes=[Re])
                    else:
                        S.op("dve", CP(eb[:, 0:n], ps), reads=[Rps], writes=[Re])
                    S.dma("sp", dst[tt * 128:(tt + 1) * 128, dc0:dc0 + n], eb[:, 0:n], reads=[Re])
                return f

            def sink_silu(dst, dc0, n):
                def f(tt, ps, Rps):
                    i = evrot[0] % 3
                    evrot[0] += 1
                    eb, Re = ev[i]
                    S.op("act", ACT(eb[:, 0:n], ps, AF.Silu), reads=[Rps], writes=[Re])
                    S.dma("sp", dst[tt * 128:(tt + 1) * 128, dc0:dc0 + n], eb[:, 0:n], reads=[Re])
                return f

            def fm_tile(wb3, Rwb, j, sink, m=128):
                for tb in range(9):
                    t0 = tb * 512
                    n = min(512, LTOK - t0)
                    pb = 2 + mrot[0] % 3
                    mrot[0] += 1
                    for kt in range(8):
                        S.op("pe", MM(PB[pb][0:m, 0:n], wb3[:, kt, j * 128:j * 128 + m], uT3[:, kt, t0:t0 + n], kt == 0, kt == 7),
                             reads=[RuT, Rwb], writes=[PBR[pb]])
                    sink(t0, n, PB[pb][0:m, 0:n], PBR[pb])

            trot = [0]

            def k_to_tm(src, Rsrc, dst_d, dcol):
                dst3 = dst_d.rearrange("(t p) d -> p t d", p=128)
                for g0 in range(0, NT, 8):
                    ng = min(8, NT - g0)
                    pb = trot[0] % 2
                    stg, Rst = tstage[trot[0] % 2]
                    trot[0] += 1
                    for i in range(ng):
                        S.op("pe", TR(PB16[pb][:, i * 128:(i + 1) * 128], src[:, (g0 + i) * 128:(g0 + i + 1) * 128], ident[:]),
                             reads=[Rsrc, Rc], writes=[PBR[pb]])
                    S.op("act", ACT(stg[:, 0:ng * 128], PB16[pb][:, 0:ng * 128], AF.Copy), reads=[PBR[pb]], writes=[Rst])
                    S.dma("sp", dst3[:, g0:g0 + ng, dcol:dcol + 128], stg[:, 0:ng * 128].rearrange("p (t d) -> p t d", d=128),
                          reads=[Rst])

            if kind == "mlstm":
                for g in range(4):
                    tm_group(2048 + g * 512, 512, sink_store(v_d, g * 512, 512))
                for g in range(4):
                    tm_group(4096 + g * 512, 512, sink_silu(z_d, g * 512, 512))
                tm_group(6144, 16, sink_store(gates_d, 0, 16, dt_bf16=False))
                preb = [A.f32(LTOK, "pre%d" % i) for i in range(2)]
                acc, Racc = A.f32(LTOK, "acc")
                outb = [A.bf16(LTOK, "outb%d" % i) for i in range(2)]
                cw, Rcw = A.f32(16 * 9, "cw")
                cb, Rcb = A.f32(16, "cb")
                S.dma("sp", cw, Wl["convw"][:, :], writes=[Rcw])
                S.dma("sp", cb, Wl["convb"][:, :], writes=[Rcb])
                acc3 = acc[:, NCTX:].rearrange("p (r c) -> p r c", c=64)
                wcur = {}

                def do_fm(dt):
                    g, j = divmod(dt, 4)
                    if g not in wcur:
                        wcur.clear()
                        wcur[g] = load_w(g * 512, 512)
                    wb3, Rwb = wcur[g]
                    pbuf, Rpbuf = preb[dt % 2]

                    def sink_pre(t0, n, ps, Rps):
                        S.op("act", ACT(pbuf[:, t0:t0 + n], ps, AF.Copy), reads=[Rps], writes=[Rpbuf])
                    fm_tile(wb3, Rwb, j, sink_pre)

                do_fm(0)
                for dt in range(16):
                    if dt + 1 < 16:
                        do_fm(dt + 1)
                    pre, Rpre = preb[dt % 2]
                    pre3 = pre[:, NCTX:].rearrange("p (r c) -> p r c", c=64)
                    tap = lambda a, b: cw[:, dt * 9 + a * 3 + b: dt * 9 + a * 3 + b + 1]
                    S.op("dve", TS(acc, pre, tap(1, 1), ALU.mult, cb[:, dt:dt + 1], ALU.add), reads=[Rpre, Rcw, Rcb], writes=[Racc])
                    S.op("dve", STT(acc[:, 1:NCTX], pre[:, 0:NCTX - 1], tap(1, 0), acc[:, 1:NCTX], ALU.mult, ALU.add),
                         reads=[Rpre, Racc], writes=[Racc])
                    S.op("dve", STT(acc[:, 0:NCTX - 1], pre[:, 1:NCTX], tap(1, 2), acc[:, 0:NCTX - 1], ALU.mult, ALU.add),
                         reads=[Rpre, Racc], writes=[Racc])
                    for a in range(3):
                        for b in range(3):
                            if a == 1 and b == 1:
                                continue
                            dr, dc = a - 1, b - 1
                            r0, r1 = max(0, -dr), 64 - max(0, dr)
                            c0, c1 = max(0, -dc), 64 - max(0, dc)
                            S.op("dve", STT(acc3[:, r0:r1, c0:c1], pre3[:, r0 + dr:r1 + dr, c0 + dc:c1 + dc], tap(a, b),
                                            acc3[:, r0:r1, c0:c1], ALU.mult, ALU.add), reads=[Rpre, Racc], writes=[Racc])
                    ob, Rob = outb[dt % 2]
                    S.op("act", ACT(ob, acc, AF.Silu), reads=[Racc], writes=[Rob])
                    if dt < 8:
                        S.dma("sp", qT_d[0][dt * 128:(dt + 1) * 128, :], ob, reads=[Rob])
                    else:
                        S.dma("sp", kT_d[0][(dt - 8) * 128:(dt - 7) * 128, :], ob, reads=[Rob])
                        k_to_tm(ob, Rob, ktm_d[0], (dt - 8) * 128)
            elif kind == "ret":
                for g in range(4):
                    tm_group(2048 + g * 512, 512, sink_store(v_d, g * 512, 512))
                for g in range(4):
                    tm_group(4096 + g * 512, 512, sink_silu(z_d, g * 512, 512))
                cosT, Rcos = A.f32(32 * 128, "cos")
                sinT, Rsin = A.f32(32 * 128, "sin")
                S.dma("sp", cosT, cos_d[:, :], writes=[Rcos])
                S.dma("sp", sinT, sin_d[:, :], writes=[Rsin])
                rsb = [A.f32(512, "rsb%d" % i) for i in range(2)]
                rt = [A.f32(256, "rt%d" % i) for i in range(4)]
                rot = [A.bf16(512, "rot%d" % i) for i in range(2)]
                rcount = [0]
                rpend = [None]
                for g in range(4):
                    isq = g < 2

                    def sink_rope(tt, ps, Rps, g=g, isq=isq):
                        i = rcount[0] % 2
                        rcount[0] += 1
                        rb, Rrb = rot[i]
                        if tt < 2:
                            S.op("act", ACT(rb, ps, AF.Copy), reads=[Rps], writes=[Rrb])
                        else:
                            sbf, Rsb = rsb[i]
                            S.op("act", ACT(sbf, ps, AF.Copy), reads=[Rps], writes=[Rsb])
                            k = tt - 2
                            sv = sbf.rearrange("p (h s w i) -> p h s w i", h=2, s=2, w=2)
                            rv = rb.rearrange("p (h s w i) -> p h s w i", h=2, s=2, w=2)
                            u1, u2 = sv[:, :, :, 0, :], sv[:, :, :, 1, :]
                            cs = cosT[:, k * 128:(k + 1) * 128].rearrange("p (s i) -> p s i", i=64).unsqueeze(1).to_broadcast([128, 2, 2, 64])
                            sn = sinT[:, k * 128:(k + 1) * 128].rearrange("p (s i) -> p s i", i=64).unsqueeze(1).to_broadcast([128, 2, 2, 64])
                            t = [rt[j][0].rearrange("p (h s i) -> p h s i", h=2, s=2) for j in range(4)]
                            Rt = [rt[j][1] for j in range(4)]
                            S.op("dve", TT(t[0], u1, cs, ALU.mult), reads=[Rsb, Rcos], writes=[Rt[0]])
                            S.op("pool", TT(t[1], u2, sn, ALU.mult), reads=[Rsb, Rsin], writes=[Rt[1]])
                            S.op("dve", TT(rv[:, :, :, 0, :], t[0], t[1], ALU.subtract), reads=[Rt[0], Rt[1]], writes=[Rrb])
                            S.op("pool", TT(t[2], u1, sn, ALU.mult), reads=[Rsb, Rsin], writes=[Rt[2]])
                            S.op("dve", TT(t[3], u2, cs, ALU.mult), reads=[Rsb, Rcos], writes=[Rt[3]])
                            S.op("dve", TT(rv[:, :, :, 1, :], t[2], t[3], ALU.add), reads=[Rt[2], Rt[3]], writes=[Rrb])
                        prev = rpend[0]
                        rpend[0] = lambda rb=rb, Rrb=Rrb, tt=tt: rope_post(rb, Rrb, tt, g, isq)
                        if prev is not None:
                            prev()

                    def rope_post(rb, Rrb, tt, g, isq):
                        pb = trot[0] % 2
                        stg, Rst = tstage[trot[0] % 2]
                        trot[0] += 1
                        for j in range(4):
                            S.op("pe", TR(PB16[pb][:, j * 128:(j + 1) * 128], rb[:, j * 128:(j + 1) * 128], ident[:]),
                                 reads=[Rrb, Rc], writes=[PBR[pb]])
                        S.op("act", ACT(stg[:, 0:512], PB16[pb][:, 0:512], AF.Copy), reads=[PBR[pb]], writes=[Rst])
                        dstT = (qT_d[0] if isq else kT_d[0]).rearrange("(a p) t -> p a t", p=128)
                        a0 = (g % 2) * 4
                        S.dma("sp", dstT[:, a0:a0 + 4, tt * 128:(tt + 1) * 128], stg[:, 0:512].rearrange("p (a t) -> p a t", t=128),
                              reads=[Rst])
                        if not isq:
                            S.dma("sp", ktm_d[0][tt * 128:(tt + 1) * 128, (g - 2) * 512:(g - 1) * 512], rb, reads=[Rrb])
                    tm_group(g * 512, 512, sink_rope)
                    if rpend[0] is not None:
                        rpend[0]()
                        rpend[0] = None
            else:
                for g in range(4):
                    tm_group(1024 + g * 512, 512, sink_store(v_d, g * 512, 512))
                for g in range(4):
                    tm_group(3072 + g * 512, 512, sink_silu(z_d, g * 512, 512))
                fst = [A.f32(512, "fst%d" % i) for i in range(2)]
                frot = [0]

                def sink_fm_store(dst_rows):
                    def f(t0, n, ps, Rps):
                        i = frot[0] % 2
                        frot[0] += 1
                        fb, Rfb = fst[i]
                        m = dst_rows.shape[0]
                        S.op("act", ACT(fb[0:m, 0:n], ps, AF.Copy), reads=[Rps], writes=[Rfb])
                        S.dma("sp", dst_rows[:, t0:t0 + n], fb[0:m, 0:n], reads=[Rfb])
                    return f
                wb3, Rwb = load_w(5120, 32)
                fm_tile(wb3, Rwb, 0, sink_fm_store(code_d[0:32, :]), m=32)
                for g in range(2):
                    wb3, Rwb = load_w(g * 512, 512)
                    for j in range(4):
                        r0 = (g * 4 + j) * 128
                        fm_tile(wb3, Rwb, j, sink_fm_store(qkraw_d[r0:r0 + 128, :]))
                S.barrier()
                A.reset()
                qf, Rqf = A.f32(LTOK, "qf")
                kf, Rkf = A.f32(LTOK, "kf")
                l1, Rl1 = A.f32(LTOK, "l1")
                Lc, RLc = A.f32(LTOK, "Lc")
                Dt, RDt = A.f32(LTOK, "Dt")
                Et, REt = A.f32(LTOK, "Et")
                codeT = [A.f32(LTOK, "code%d" % i) for i in range(2)]
                go = [A.bf16(LTOK, "go%d" % i) for i in range(3)]
                w2sb, Rw2 = A.f32(1024, "w2")
                gkb, Rgkb = A.f32(8, "gkb")
                ngkb, Rngkb = A.f32(8, "ngkb")
                tstage = [A.bf16(8 * 128, "tstage%d" % i) for i in range(2)]
                S.dma("sp", w2sb[0:16, :], Wl["w2"][:, :], writes=[Rw2])
                S.dma("sp", gkb, Wl["gkb"][:, :], writes=[Rgkb])
                S.op("dve", TS(ngkb, gkb, -1.0, ALU.mult), reads=[Rgkb], writes=[Rngkb])
                for dirn in range(2):
                    S.dma("sp", codeT[dirn][0][0:16, :], code_d[dirn * 16:(dirn + 1) * 16, :], writes=[codeT[dirn][1]])
                Lc3 = Lc.rearrange("p (c k) -> p c k", k=128)
                l13 = l1.rearrange("p (c k) -> p c k", k=128)
                Dt3 = Dt.rearrange("p (c k) -> p c k", k=128)
                tot_b = Lc3[:, :, 127:128].to_broadcast([128, NT, 128])
                gdec4 = gdec[:].rearrange("p (d h c) -> p d h c", d=2, h=4)
                for h in range(4):
                    S.dma("sp", qf, qkraw_d[h * 128:(h + 1) * 128, :], writes=[Rqf])
                    S.dma("sp", kf, qkraw_d[512 + h * 128:512 + (h + 1) * 128, :], writes=[Rkf])
                    for dirn in range(2):
                        cT, RcT = codeT[dirn]
                        for tb in range(9):
                            t0 = tb * 512
                            n = min(512, LTOK - t0)
                            pb = 2 + mrot[0] % 3
                            mrot[0] += 1
                            S.op("pe", MM(PB[pb][:, 0:n], w2sb[0:16, dirn * 512 + h * 128: dirn * 512 + (h + 1) * 128], cT[0:16, t0:t0 + n], True, True),
                                 reads=[Rw2, RcT], writes=[PBR[pb]])
                            S.op("act", ACT(l1[:, t0:t0 + n], PB[pb][:, 0:n], AF.Exp, scale=-1.0, bias=ngkb[:, dirn * 4 + h: dirn * 4 + h + 1]),
                                 reads=[PBR[pb], Rngkb], writes=[Rl1])
                        S.op("act", ACT(l1, l1, AF.Ln, bias=one_c), reads=[Rl1, Rc], writes=[Rl1])
                        for c in range(NT):
                            S.op("dve", lambda e, c=c: e.tensor_tensor_scan(out=Lc[:, c * 128:(c + 1) * 128], data0=onesF[:],
                                                                             data1=l1[:, c * 128:(c + 1) * 128], initial=0.0,
                                                                             op0=ALU.mult, op1=ALU.add),
                                 reads=[Rl1, Rc], writes=[RLc])
                        S.op("act", ACT(gdec4[:, dirn, h, :], Lc3[:, :, 127], AF.Exp, scale=-1.0 / 16), reads=[RLc], writes=[Res()])
                        S.op("dve", TT(Dt3, tot_b, Lc3, ALU.subtract), reads=[RLc], writes=[RDt])
                        if dirn == 0:
                            eq, ek, ekh = Lc, Lc, Dt
                            Req, Rek, Rekh = RLc, RLc, RDt
                        else:
                            S.op("dve", TT(Dt, Dt, l1, ALU.add), reads=[RDt, Rl1], writes=[RDt])
                            S.op("dve", TT(Lc, Lc, l1, ALU.subtract), reads=[RLc, Rl1], writes=[RLc])
                            eq, ek, ekh = Dt, Dt, Lc
                            Req, Rek, Rekh = RDt, RDt, RLc
                        oq, Roq = go[0]
                        ok, Rok = go[1]
                        okh, Rokh = go[2]
                        S.op("act", ACT(Et, eq, AF.Exp, scale=-1.0 / 16, bias=lqs_c), reads=[Req, Rc], writes=[REt])
                        S.op("dve", TT(oq, qf, Et, ALU.mult), reads=[Rqf, REt], writes=[Roq])
                        S.dma("sp", qT_d[dirn][h * 128:(h + 1) * 128, :], oq, reads=[Roq])
                        S.op("act", ACT(Et, ek, AF.Exp, scale=1.0 / 16), reads=[Rek, Roq], writes=[REt])
                        S.op("dve", TT(ok, kf, Et, ALU.mult), reads=[Rkf, REt], writes=[Rok])
                        S.dma("sp", kT_d[dirn][h * 128:(h + 1) * 128, :], ok, reads=[Rok])
                        S.op("act", ACT(Et, ekh, AF.Exp, scale=-1.0 / 16), reads=[Rekh, Rok], writes=[REt])
                        S.op("dve", TT(okh, kf, Et, ALU.mult), reads=[Rkf, REt], writes=[Rokh])
                        k_to_tm(okh, Rokh, ktm_d[dirn], h * 128)
            S.barrier()

            if os.environ.get("BACKSKIP"):
                continue
            A.reset()
            ndt = 2 if scalar_mix else 1
            dk = 128 * ndt
            SW = 516
            S32, RS32 = A.f32(4 * 2 * SW, "S32")
            S32v = S32.rearrange("p (h d w) -> p h d w", h=4, d=2)
            S16 = [A.bf16(4 * 2 * SW, "S16_%d" % i) for i in range(2)]
            S16v = [s[0].rearrange("p (h d w) -> p h d w", h=4, d=2) for s in S16]
            RS16 = [[[Res() for _ in range(2)] for _ in range(4)] for _ in range(2)]
            RS32h = [[Res() for _ in range(2)] for _ in range(4)]
            qc = [A.bf16(8 * 128, "qc%d" % i) for i in range(2)]
            kc = [A.bf16(8 * 128, "kc%d" % i) for i in range(2)]
            ktc = [A.bf16(1024, "ktc%d" % i) for i in range(2)]
            vc = [A.bf16(DI, "vc%d" % i) for i in range(2)]
            ku = [A.bf16(1024, "ku%d" % i) for i in range(2)]
            sT16 = [A.bf16(512, "sT%d" % i) for i in range(2)]
            RsTh = [[Res() for _ in range(4)] for _ in range(2)]
            Rkuh = [[Res() for _ in range(4)] for _ in range(2)]
            osb = [A.f32(DI, "osb%d" % i) for i in range(2)]
            Rosbh = [[Res() for _ in range(4)] for _ in range(2)]
            ofwb = [A.f32(DI, "ofw%d" % i) for i in range(2)]
            zcb = [A.bf16(DI, "zc%d" % i) for i in range(2)]
            holdb = [A.f32(D, "hold%d" % i) for i in range(3)]
            szb, Rsz = A.f32(DI, "sz")
            yb, Ryb = A.bf16(DI, "y")
            yT, RyT = A.bf16(16 * 128, "yT")
            hnew, Rhnew = A.f32(D, "hnew")
            ptmp, Rptmp = A.f32(512, "ptmp")
            gbc, Rg = A.f32(D, "gbc")
            hnwbc, Rhnw = A.f32(DI, "hnw")
            ow16, Row = A.bf16(16 * 1024, "ow16")
            ow16v = ow16.rearrange("p (f n) -> p f n", n=1024)
            st8, Rst8 = A.f32(8, "st8")
            sfw, _ = A.f32(NT * 4, "sfw")
            Rsfw = [Res() for _ in range(NT)]
            sbw = [A.f32(4, "sbw%d" % i) for i in range(2)]
            mean4, Rmean = A.f32(4, "mean")
            msq4, Rmsq = A.f32(4, "msq")
            var4, Rvar = A.f32(4, "var")
            rstd4, Rrstd = A.f32(4, "rstd")
            ttmp, Rttmp = A.f32(512, "ttmp")
            fss, Rfss = A.f32(1, "fss")
            frs, Rfrs = A.f32(1, "frs")
            dna, Rdna = A.f32(4, "dna")
            sc2, Rsc2 = A.f32(4, "sc2")
            Rdnah = [Res() for _ in range(4)]
            Rsc2h = [Res() for _ in range(4)]
            NS = NT * 8
            gt, Rgt = A.f32(NT * 16, "gates")
            gbb, Rgbb = A.f32(16, "gate_b")
            nlf, Rnlf = A.f32(NS, "nlf")
            i8, Ri8 = A.f32(NS, "i8")
            cum, Rcum = A.f32(NS, "cum")
            tot, Rtot = A.f32(NS, "tot")
            eB, ReB = A.f32(NS, "eB")
            emB, RemB = A.f32(NS, "emB")
            wsc, Rwsc = A.f32(NS, "wsc")
            usc, Rusc = A.f32(NS, "usc")
            dec, Rdec = A.f32(NS, "dec")
            stmp, Rstmp = A.f32(NS, "stmp")
            owst, Rowst = A.f32(2 * 1024, "owst")
            fnwbc, Rfnw = owst[:, 0:D], Rowst

            ow3 = Wl["out_w"].rearrange("(f p) n -> p f n", p=128)
            owst3 = owst.rearrange("p (f n) -> p f n", n=1024)
            S.dma("sp", hnwbc[:, 0:16], Wl["hnwT"][:, :], writes=[Rhnw])
            for i in range(8):
                S.dma("sp", owst3, ow3[:, 2 * i:2 * i + 2, :], writes=[Rowst])
                for j in range(2):
                    f = 2 * i + j
                    S.op("act", ACT(ow16v[:, f, :], owst3[:, j, :], AF.Copy, scale=hnwbc[:, f:f + 1]), reads=[Rowst, Rhnw], writes=[Row])
            if last:
                S.dma("sp", fnwbc, fnw_d.partition_broadcast(128), writes=[Rfnw])

            if scalar_mix:
                nch = NT if kind == "mlstm" else 1
                nlf3 = nlf.rearrange("p (c g) -> p c g", g=8)
                i83 = i8.rearrange("p (c g) -> p c g", g=8)
                if kind == "mlstm":
                    gt3 = gt.rearrange("p (c g) -> p c g", g=16)
                    gd3 = gates_d.rearrange("(c p) g -> p c g", p=128)
                    S.dma("sp", gt3[:, 0:17, :], gd3[:, 0:17, :], writes=[Rgt])
                    S.dma("sp", gt3[:, 17:NT, :], gd3[:, 17:NT, :], writes=[Rgt])
                    S.dma("sp", gbb, Wl["gate_b"].partition_broadcast(128), writes=[Rgbb])
                    S.op("dve", TT(gt3, gt3, gbb.unsqueeze(1).to_broadcast([128, NT, 16]), ALU.add), reads=[Rgt, Rgbb], writes=[Rgt])
                    for dirn in range(2):
                        S.op("act", ACT(nlf3[:, :, dirn * 4:dirn * 4 + 4], gt3[:, :, dirn * 8 + 4:dirn * 8 + 8], AF.Exp, scale=-1.0),
                             reads=[Rgt], writes=[Rnlf])
                        S.op("dve", CP(i83[:, :, dirn * 4:dirn * 4 + 4], gt3[:, :, dirn * 8:dirn * 8 + 4]), reads=[Rgt], writes=[Ri8])
                    S.op("act", ACT(nlf, nlf, AF.Ln, bias=one_c), reads=[Rnlf, Rc], writes=[Rnlf])
                else:
                    S.dma("sp", gbb[:, 0:8], Wl["decay"].partition_broadcast(128), writes=[Rgbb])
                    S.op("act", ACT(nlf[:, 0:8], gbb[:, 0:8], AF.Exp, scale=-1.0), reads=[Rgbb], writes=[Rnlf])
                    S.op("act", ACT(nlf[:, 0:8], nlf[:, 0:8], AF.Ln, bias=one_c), reads=[Rnlf, Rc], writes=[Rnlf])
                    S.op("dve", MSET(i8[:, 0:8], 0.0), writes=[Ri8])
                for c in range(nch):
                    S.op("pe", MM(PB[0][:, c * 8:c * 8 + 4], maskF[:], nlf[:, c * 8:c * 8 + 4], True, True), reads=[Rnlf, Rc], writes=[PBR[0]])
                    S.op("pe", MM(PB[0][:, c * 8 + 4:c * 8 + 8], maskB[:], nlf[:, c * 8 + 4:c * 8 + 8], True, True), reads=[Rnlf, Rc], writes=[PBR[0]])
                    S.op("pe", MM(PB[1][:, c * 8:c * 8 + 8], onesF[:], nlf[:, c * 8:c * 8 + 8], True, True), reads=[Rnlf, Rc], writes=[PBR[1]])
                n8 = nch * 8
                S.op("dve", CP(cum[:, 0:n8], PB[0][:, 0:n8]), reads=[PBR[0]], writes=[Rcum])
                S.op("dve", CP(tot[:, 0:n8], PB[1][:, 0:n8]), reads=[PBR[1]], writes=[Rtot])
                S.op("act", ACT(eB[:, 0:n8], cum[:, 0:n8], AF.Exp, scale=-1.0), reads=[Rcum], writes=[ReB])
                S.op("act", ACT(emB[:, 0:n8], cum[:, 0:n8], AF.Exp), reads=[Rcum], writes=[RemB])
                S.op("dve", TT(stmp[:, 0:n8], i8[:, 0:n8], cum[:, 0:n8], ALU.add), reads=[Ri8, Rcum], writes=[Rstmp])
                S.op("act", ACT(wsc[:, 0:n8], stmp[:, 0:n8], AF.Exp, bias=nl16_c), reads=[Rstmp, Rc], writes=[Rwsc])
                S.op("dve", TT(stmp[:, 0:n8], stmp[:, 0:n8], tot[:, 0:n8], ALU.subtract), reads=[Rstmp, Rtot, Rwsc], writes=[Rstmp])
                S.op("act", ACT(usc[:, 0:n8], stmp[:, 0:n8], AF.Exp, bias=nl16_c), reads=[Rstmp, Rc], writes=[Rusc])
                S.op("act", ACT(dec[:, 0:n8], tot[:, 0:n8], AF.Exp, scale=-1.0), reads=[Rtot], writes=[Rdec])

            S.barrier()

            def scol(tab, c, dirn, h):
                cc = c if kind == "mlstm" else 0
                o = cc * 8 + dirn * 4 + h
                return tab[:, o:o + 1]

            gdec4 = gdec[:].rearrange("p (d h c) -> p d h c", d=2, h=4)
            nq = 4 * ndt
            Rofw_d = [Res() for _ in range(NT)]
            Rh_d = [Res() for _ in range(NT)]
            PQ = [Res() for _ in range(4)]
            PD = [Res() for _ in range(4)]
            PN = [[Res() for _ in range(2)] for _ in range(4)]
            steps = [(0, c) for c in range(NT)] + [(1, c) for c in [1, 0] + list(range(NT - 1, 1, -1))]
            g_loaded = [None]

            def do_p5(c):
                return not (last and c < 2)

            def issue_loads(g):
                dirn, c = steps[g]
                bi = g % 2
                src = 0 if scalar_mix else dirn
                qsrc = qT_d[src].rearrange("(a p) t -> p a t", p=128)
                ksrc = kT_d[src].rearrange("(a p) t -> p a t", p=128)
                qb3 = qc[bi][0].rearrange("p (a t) -> p a t", t=128)
                kb3 = kc[bi][0].rearrange("p (a t) -> p a t", t=128)
                S.dma("sp", kb3[:, 0:nq, :], ksrc[:, 0:nq, c * 128:(c + 1) * 128], writes=[kc[bi][1]])
                S.dma("sp", qb3[:, 0:nq, :], qsrc[:, 0:nq, c * 128:(c + 1) * 128], writes=[qc[bi][1]])
                S.dma("sp", vc[bi][0], v_d[c * 128:(c + 1) * 128, :], writes=[vc[bi][1]])
                S.dma("sp", ktc[bi][0][:, 0:4 * dk], ktm_d[src][c * 128:(c + 1) * 128, 0:4 * dk], writes=[ktc[bi][1]])

            def issue_p5_loads(g):
                if g >= len(steps):
                    return
                dirn, c = steps[g]
                bi = g % 2
                if dirn == 1 and do_p5(c):
                    S.dma("sp", ofwb[bi][0], ofw_d[c * 128:(c + 1) * 128, :], reads=[Rofw_d[c]], writes=[ofwb[bi][1]])
                    S.dma("sp", zcb[bi][0], z_d[c * 128:(c + 1) * 128, :], writes=[zcb[bi][1]])

            def issue_hold_load(g):
                if not p5_active(g):
                    return
                c = steps[g][1]
                bi = g % 3
                S.dma("sp", holdb[bi][0], h_src(c), reads=[Rh_d[c]], writes=[holdb[bi][1]])

            center = kind != "gla"

            def p5_active(g):
                return 0 <= g < len(steps) and steps[g][0] == 1 and do_p5(steps[g][1])

            def p5_A1(g):
                if not p5_active(g):
                    return
                bi = g % 2
                ob = osb[bi][0]
                ofw, Rofw = ofwb[bi]
                Ro4 = Rosbh[bi]
                for h in range(4):
                    oh = ob[:, h * 512:(h + 1) * 512]
                    S.op("dve", TT(oh, oh, ofw[:, h * 512:(h + 1) * 512], ALU.add), reads=[Ro4[h], Rofw], writes=[Ro4[h]])

            def p5_A1act(g):
                if not p5_active(g):
                    return
                bi = g % 2
                ob = osb[bi][0]
                Ro4 = Rosbh[bi]
                for h in range(4):
                    oh = ob[:, h * 512:(h + 1) * 512]
                    S.op("act", ACT(ttmp, oh, AF.Square, accum=st8[:, 4 + h:5 + h]), reads=[Ro4[h]], writes=[Rttmp, Rst8])

            def p5_mid(g):
                if not p5_active(g):
                    return
                if center:
                    cg = steps[g][1]
                    S.op("dve", TT(mean4, sbw[g % 2][0], sfw[:, cg * 4:cg * 4 + 4], ALU.add), reads=[sbw[g % 2][1], Rsfw[cg]], writes=[Rmean])
                    S.op("dve", TS(mean4, mean4, 1.0 / 512, ALU.mult), reads=[Rmean], writes=[Rmean])
                    S.op("dve", TT(msq4, mean4, mean4, ALU.mult), reads=[Rmean], writes=[Rmsq])
                    S.op("dve", STT(var4, st8[:, 4:8], 1.0 / 512, msq4, ALU.mult, ALU.subtract), reads=[Rst8, Rmsq], writes=[Rvar])
                    S.op("act", ACT(rstd4, var4, AF.Ln, bias=eps_c), reads=[Rvar, Rc], writes=[Rrstd])
                else:
                    S.op("act", ACT(rstd4, st8[:, 4:8], AF.Ln, scale=1.0 / 512, bias=eps_c), reads=[Rst8, Rc], writes=[Rrstd])
                S.op("act", ACT(rstd4, rstd4, AF.Exp, scale=-0.5), reads=[Rrstd], writes=[Rrstd])

            def p5_A2(g):
                if not p5_active(g):
                    return
                bi = g % 2
                ob = osb[bi][0]
                Ro4 = Rosbh[bi]
                szb, Rsz = zcb[bi]
                for h in range(4):
                    oh = ob[:, h * 512:(h + 1) * 512]
                    zh = szb[:, h * 512:(h + 1) * 512]
                    yh = yb[:, h * 512:(h + 1) * 512]
                    if center:
                        S.op("dve", STT(oh, oh, mean4[:, h:h + 1], zh, ALU.subtract, ALU.mult), reads=[Ro4[h], Rmean, Rsz], writes=[Ro4[h]])
                        S.op("act", ACT(yh, oh, AF.Copy, scale=rstd4[:, h:h + 1]), reads=[Ro4[h], Rrstd], writes=[Ryb])
                    else:
                        S.op("dve", STT(yh, oh, rstd4[:, h:h + 1], zh, ALU.mult, ALU.mult), reads=[Ro4[h], Rrstd, Rsz], writes=[Ryb])

            def p5_B1(g):
                if not p5_active(g):
                    return
                for half in range(2):
                    pb = half
                    for i in range(8):
                        f = half * 8 + i
                        S.op("pe", TR(PB16[pb][:, i * 128:(i + 1) * 128], yb[:, f * 128:(f + 1) * 128], ident[:]),
                             reads=[Ryb, Rc], writes=[PBR[pb]])
                    S.op("act", ACT(yT[:, half * 1024:(half + 1) * 1024], PB16[pb], AF.Copy), reads=[PBR[pb]], writes=[RyT])

            def p5_B2a(g):
                if not p5_active(g):
                    return
                yT3 = yT.rearrange("p (f t) -> p f t", t=128)
                for nb in range(2):
                    pb = 6 + nb
                    for f in range(16):
                        S.op("pe", MM(PB[pb], yT3[:, f, :], ow16v[:, f, nb * 512:(nb + 1) * 512], f == 0, f == 15),
                             reads=[RyT, Row], writes=[PBR[pb]])

            def p5_B2b(g):
                if not p5_active(g):
                    return
                c = steps[g][1]
                bi = g % 3
                hold, Rhold = holdb[bi]
                want = 1 if c < 2 else 0
                if g_loaded[0] != want:
                    g_loaded[0] = want
                    S.dma("sp", gbc, ada_d[li, want, 2 * D:3 * D].partition_broadcast(128), writes=[Rg])
                for nb in range(2):
                    pb = 6 + nb
                    S.op("dve", TT(ptmp, PB[pb], gbc[:, nb * 512:(nb + 1) * 512], ALU.mult), reads=[PBR[pb], Rg], writes=[Rptmp])
                    S.op("dve", TT(hnew[:, nb * 512:(nb + 1) * 512], ptmp, hold[:, nb * 512:(nb + 1) * 512], ALU.add),
                         reads=[Rptmp, Rhold], writes=[Rhnew])
                if not last:
                    S.dma("pool", h_d[c * 128:(c + 1) * 128, :], hnew, reads=[Rhnew], writes=[Rh_d[c]])
                else:
                    S.op("act", ACT(ttmp[:, 0:512], hnew[:, 0:512], AF.Square, accum=fss), reads=[Rhnew], writes=[Rttmp, Rfss])
                    S.op("act", ACT(ttmp[:, 0:512], hnew[:, 512:1024], AF.Square, accum=frs), reads=[Rhnew], writes=[Rttmp, Rfrs])
                    S.op("dve", TT(fss, fss, frs, ALU.add), reads=[Rfss, Rfrs], writes=[Rfss])
                    S.op("act", ACT(frs, fss, AF.Ln, scale=1.0 / D, bias=eps_c), reads=[Rfss, Rc], writes=[Rfrs])
                    S.op("act", ACT(frs, frs, AF.Exp, scale=-0.5), reads=[Rfrs], writes=[Rfrs])
                    S.op("dve", STT(hold, hnew, frs, fnwbc, ALU.mult, ALU.mult), reads=[Rhnew, Rfrs, Rfnw], writes=[Rhold])
                    final_ops.append(S.dma("pool", out_d[(c - 2) * 128:(c - 1) * 128, :], hold, reads=[Rhold]))

            def emit_head_a(g):
                if g >= len(steps):
                    return
                dirn, c = steps[g]
                ls = g if dirn == 0 else g - NT
                if ls == 0:
                    allS = [RS32h[h][d] for h in range(4) for d in range(2)]
                    S.op("pool", MSET(S32, 0.0), writes=[RS32] + allS)
                    for b in range(2):
                        S.op("pool", MSET(S16[b][0], 0.0), writes=[RS16[b][h][d] for h in range(4) for d in range(2)])
                mask = maskF if dirn == 0 else maskB
                bi = g % 2
                cur, nxt = ls % 2, (ls + 1) % 2
                qb, Rq = qc[bi]
                kb, Rk = kc[bi]
                ktb, Rkt = ktc[bi]
                vb, Rv = vc[bi]
                kub = ku[bi][0]
                sTb = sT16[bi][0]
                ob = osb[bi][0]
                qb3 = qb.rearrange("p (a t) -> p a t", t=128)
                kb3 = kb.rearrange("p (a t) -> p a t", t=128)
                for h in range(4):
                    sb_ = h % 2
                    sps = PB[sb_][:, 0:128]
                    for d2 in range(ndt):
                        S.op("pe", MM(sps, kb3[:, h * ndt + d2, :], qb3[:, h * ndt + d2, :], d2 == 0, d2 == ndt - 1),
                             reads=[Rk, Rq], writes=[PBR[sb_]])
                    sTh = sTb[:, h * 128:(h + 1) * 128]
                    if scalar_mix:
                        S.op("dve", STT(sTh, sps, scol(wsc, c, dirn, h), mask[:], ALU.mult, ALU.mult),
                             reads=[PBR[sb_], Rwsc, Rc], writes=[RsTh[bi][h]])
                    else:
                        S.op("dve", TT(sTh, sps, mask[:], ALU.mult), reads=[PBR[sb_], Rc], writes=[RsTh[bi][h]])

            def emit_head_b(g):
                if g >= len(steps):
                    return
                dirn, c = steps[g]
                ls = g if dirn == 0 else g - NT
                mask = maskF if dirn == 0 else maskB
                bi = g % 2
                cur, nxt = ls % 2, (ls + 1) % 2
                qb, Rq = qc[bi]
                kb, Rk = kc[bi]
                ktb, Rkt = ktc[bi]
                vb, Rv = vc[bi]
                kub = ku[bi][0]
                sTb = sT16[bi][0]
                ob = osb[bi][0]
                qb3 = qb.rearrange("p (a t) -> p a t", t=128)
                kb3 = kb.rearrange("p (a t) -> p a t", t=128)
                if kind == "mlstm":
                    for h in range(4):
                        sTh = sTb[:, h * 128:(h + 1) * 128]
                        dps = PB[6][:, h:h + 1]
                        S.op("pe", MM(dps, sTh, ones16[:, 0:1], True, False), reads=[RsTh[bi][h], Rc], writes=[PBR[6]])
                        for d2 in range(ndt):
                            S.op("pe", MM(dps, qb3[:, h * ndt + d2, :], S16v[cur][:, h, d2, 512:513], False, d2 == ndt - 1),
                                 reads=[Rq, RS16[cur][h][d2]], writes=[PBR[6]])
                    o8 = (c * 8 + dirn * 4)
                    S.op("act", ACT(dna, PB[6][:, 0:4], AF.Abs), reads=[PBR[6]], writes=[Rdna])
                    S.op("dve", TT(dna, dna, emB[:, o8:o8 + 4], ALU.max), reads=[Rdna, RemB], writes=[Rdna])
                    S.op("dve", lambda e: e.reciprocal(out=sc2, in_=dna), reads=[Rdna], writes=[Rsc2])
                if scalar_mix:
                    for h in range(4):
                        S.op("dve", TS(kub[:, h * dk:(h + 1) * dk], ktb[:, h * dk:(h + 1) * dk], scol(usc, c, dirn, h), ALU.mult),
                             reads=[Rkt, Rusc], writes=[Rkuh[bi][h]])

            def emit_body(g):
                dirn, c = steps[g]
                ls = g if dirn == 0 else g - NT
                mask = maskF if dirn == 0 else maskB
                bi = g % 2
                cur, nxt = ls % 2, (ls + 1) % 2
                qb, Rq = qc[bi]
                kb, Rk = kc[bi]
                ktb, Rkt = ktc[bi]
                vb, Rv = vc[bi]
                kub = ku[bi][0]
                sTb = sT16[bi][0]
                ob = osb[bi][0]
                qb3 = qb.rearrange("p (a t) -> p a t", t=128)
                kb3 = kb.rearrange("p (a t) -> p a t", t=128)
                for h in range(4):
                    if scalar_mix:
                        ksrc_tm, Rks = kub, Rkuh[bi][h]
                    else:
                        ksrc_tm, Rks = ktb, Rkt
                    for d2 in range(ndt):
                        dsp = 4 + (h * ndt + d2) % 2
                        lw = ksrc_tm[:, h * dk + d2 * 128: h * dk + (d2 + 1) * 128]
                        S.op("pe", MM(PB[dsp], lw, vb[:, h * 512:(h + 1) * 512], True, True), reads=[Rks, Rv], writes=[PBR[dsp]])
                        dcol = scol(dec, c, dirn, h) if scalar_mix else gdec4[:, dirn, h, c:c + 1]
                        S.op("dve", STT(S32v[:, h, d2, 0:512], S32v[:, h, d2, 0:512], dcol, PB[dsp], ALU.mult, ALU.add),
                             reads=[PBR[dsp], RS32h[h][d2], Rdec], writes=[RS32h[h][d2]])
                    if kind == "mlstm":
                        for d2 in range(ndt):
                            lw = kub[:, h * dk + d2 * 128: h * dk + (d2 + 1) * 128]
                            S.op("pe", MM(PB[7][:, 8 + h * 2 + d2:9 + h * 2 + d2], lw, ones16[:, 0:1], True, True),
                                 reads=[Rkuh[bi][h], Rc], writes=[PBR[7]])
                        for d2 in range(ndt):
                            S.op("dve", STT(S32v[:, h, d2, 512:513], S32v[:, h, d2, 512:513], scol(dec, c, dirn, h),
                                            PB[7][:, 8 + h * 2 + d2:9 + h * 2 + d2], ALU.mult, ALU.add),
                                 reads=[PBR[7], RS32h[h][d2], Rdec], writes=[RS32h[h][d2]])
                    sTh = sTb[:, h * 128:(h + 1) * 128]
                    ops_ = 2 + h % 2
                    S.op("pe", MM(PB[ops_], sTh, vb[:, h * 512:(h + 1) * 512], True, False), reads=[RsTh[bi][h], Rv], writes=[PBR[ops_]])
                    for d2 in range(ndt):
                        S.op("pe", MM(PB[ops_], qb3[:, h * ndt + d2, :], S16v[cur][:, h, d2, 0:512], False, d2 == ndt - 1),
                             reads=[Rq, RS16[cur][h][d2]], writes=[PBR[ops_]])
                    oh = ob[:, h * 512:(h + 1) * 512]
                    if dirn == 0:
                        acc_ap, acc_res = sfw[:, c * 4 + h:c * 4 + h + 1], Rsfw[c]
                    else:
                        acc_ap, acc_res = sbw[bi][0][:, h:h + 1], sbw[bi][1]
                    if kind == "mlstm":
                        S.op("act", ACT(oh, PB[ops_], AF.Copy, scale=sc2[:, h:h + 1], accum=acc_ap), reads=[PBR[ops_], Rsc2], writes=[Rosbh[bi][h], acc_res])
                    elif kind == "ret":
                        S.op("act", ACT(oh, PB[ops_], AF.Copy, scale=scol(eB, c, dirn, h), accum=acc_ap), reads=[PBR[ops_], ReB], writes=[Rosbh[bi][h], acc_res])
                    else:
                        S.op("act", ACT(oh, PB[ops_], AF.Copy), reads=[PBR[ops_]], writes=[Rosbh[bi][h]])
                    for d2 in range(ndt):
                        if True:
                            S.op("act", ACT(S16v[nxt][:, h, d2, 0:513], S32v[:, h, d2, 0:513], AF.Copy),
                                 reads=[RS32h[h][d2]], writes=[RS16[nxt][h][d2]])
                        else:
                            S.op("pool", CP(S16v[nxt][:, h, d2, 0:513], S32v[:, h, d2, 0:513]),
                                 reads=[RS32h[h][d2]], writes=[RS16[nxt][h][d2]])
                    if h == 0:
                        p5_A1(g - 1)
                    if h == 1:
                        p5_A1act(g - 1)
                    if h == 2:
                        p5_mid(g - 1)
                if dirn == 0:
                    S.dma("act", ofw_d[c * 128:(c + 1) * 128, :], ob, reads=Rosbh[bi], writes=[Rofw_d[c]])

            issue_loads(0)
            emit_head_a(0)
            emit_head_b(0)
            for g in range(len(steps)):
                if g + 1 < len(steps):
                    issue_loads(g + 1)
                emit_body(g)
                p5_A2(g - 1)
                emit_head_a(g + 1)
                p5_B2a(g - 2)
                p5_B2b(g - 2)
                emit_head_b(g + 1)
                p5_B1(g - 1)
                issue_p5_loads(g + 1)
                issue_hold_load(g)
            LS = len(steps)
            p5_B2a(LS - 2)
            p5_B2b(LS - 2)
            p5_A1(LS - 1)
            p5_A1act(LS - 1)
            p5_mid(LS - 1)
            p5_A2(LS - 1)
            p5_B1(LS - 1)
            p5_B2a(LS - 1)
            p5_B2b(LS - 1)
            S.barrier()
        S.emit(final_ops=final_ops)
    return nc


def CP_ACT(out, in_):
    return lambda e: e.activation(out=out, in_=in_, func=AF.Copy)


def _rope_tables():
    inv_freq = (np.float32(10000.0) ** (-np.arange(64, dtype=np.float32) / np.float32(64))).astype(np.float32)
    p = np.arange(128)
    cos = np.zeros((128, 32, 2, 64), np.float32)
    sin = np.zeros((128, 32, 2, 64), np.float32)
    for k in range(32):
        row = (2 * k + p // 64).astype(np.float32)
        col = (p % 64).astype(np.float32)
        for s, pos in enumerate((row, col)):
            ang = (pos[:, None] * inv_freq[None, :]).astype(np.float32)
            cos[:, k, s, :] = np.cos(ang)
            sin[:, k, s, :] = np.sin(ang)
    return cos.reshape(128, 32 * 128), sin.reshape(128, 32 * 128)


_INPUT_NAMES = (
    "x", "c", "ctx", "c_ctx",
    "l0_norm_w", "l0_ada_w", "l0_ada_b", "l0_in_w", "l0_conv_w", "l0_conv_b", "l0_gate_b", "l0_head_norm_w", "l0_out_w",
    "l1_norm_w", "l1_ada_w", "l1_ada_b", "l1_in_w", "l1_gk_w2", "l1_gk_b", "l1_head_norm_w", "l1_out_w",
    "l2_norm_w", "l2_ada_w", "l2_ada_b", "l2_in_w", "l2_decay_logit", "l2_head_norm_w", "l2_out_w",
    "l3_norm_w", "l3_ada_w", "l3_ada_b", "l3_in_w", "l3_conv_w", "l3_conv_b", "l3_gate_b", "l3_head_norm_w", "l3_out_w",
    "final_norm_w",
)


def make_in_maps(inputs, ncores=8):
    missing = [n for n in _INPUT_NAMES if n not in inputs]
    assert not missing, missing
    f = lambda a: np.ascontiguousarray(np.asarray(a, dtype=np.float32))
    shared = {}
    idx = np.arange(128)
    shared["ident16"] = np.eye(128, dtype=np.float32).astype(ml_dtypes.bfloat16)
    shared["maskF"] = (idx[None, :] >= idx[:, None]).astype(np.float32)
    shared["maskB"] = (idx[None, :] <= idx[:, None]).astype(np.float32)
    cos, sin = _rope_tables()
    shared["rope_cos"] = cos
    shared["rope_sin"] = sin
    shared["final_norm_w"] = f(inputs["final_norm_w"])
    for li in range(4):
        kind = KINDS[li]
        p = "l%d_" % li
        for nm in ("norm_w", "ada_w", "ada_b", "in_w", "out_w"):
            shared[p + nm] = f(inputs[p + nm])
        shared[p + "hnwT"] = np.ascontiguousarray(f(inputs[p + "head_norm_w"]).reshape(16, 128).T)
        if kind == "mlstm":
            cw = f(inputs[p + "conv_w"]).reshape(9, 16, 128)
            shared[p + "convw"] = np.ascontiguousarray(cw.transpose(2, 1, 0).reshape(128, 16 * 9))
            shared[p + "convb"] = np.ascontiguousarray(f(inputs[p + "conv_b"]).reshape(16, 128).T)
            shared[p + "gate_b"] = f(inputs[p + "gate_b"])
        elif kind == "gla":
            shared[p + "w2"] = np.ascontiguousarray(f(inputs[p + "gk_w2"]).transpose(1, 0, 2).reshape(16, 1024))
            gb = f(inputs[p + "gk_b"]).reshape(2, 4, 128)
            shared[p + "gkb"] = np.ascontiguousarray(gb.transpose(2, 0, 1).reshape(128, 8))
        else:
            shared[p + "decay"] = f(inputs[p + "decay_logit"]).reshape(8)
    x = f(inputs["x"])
    ctx = f(inputs["ctx"])
    c = f(inputs["c"])
    c_ctx = f(inputs["c_ctx"])
    maps = []
    for b in range(ncores):
        m = dict(shared)
        m["x"] = x[b]
        m["ctx"] = ctx[b]
        cc = np.stack([c[b], c_ctx], axis=0)
        m["ccT"] = np.ascontiguousarray(cc.reshape(2, 8, 128).transpose(2, 1, 0).reshape(128, 16))
        maps.append(m)
    return maps


_NC_CACHE = {}


def kernel(**inputs):
    if 4 not in _NC_CACHE:
        _NC_CACHE[4] = build(4)
    nc = _NC_CACHE[4]
    maps = make_in_maps(inputs, 8)
    res = run_bass_kernel_spmd(nc, maps, core_ids=list(range(8)))
    return np.stack([np.asarray(r["out"], dtype=np.float32) for r in res.results], axis=0)
```

```python
import math
import os
import contextlib
import numpy as np
import ml_dtypes
import concourse.bass as bass
import concourse.mybir as mybir
from concourse.bass_utils import run_bass_kernel_spmd

F32 = mybir.dt.float32
BF16 = mybir.dt.bfloat16
AF = mybir.ActivationFunctionType
ALU = mybir.AluOpType

LTOK = 4352
NT = 34
NCTX = 256
D = 1024
DI = 2048
EPS = 1e-6
KINDS = ("mlstm", "gla", "ret", "mlstm")
NIN = {"mlstm": 6160, "gla": 5152, "ret": 6144}
ARENA_WORDS = 51200
STRICT_WAR = os.environ.get("STRICT_WAR", "1") == "1"


class Res:
    __slots__ = ("name", "writer", "readers", "excl")

    def __init__(self, name="", excl=False):
        self.name = name
        self.writer = None
        self.readers = []
        self.excl = excl


class Op:
    __slots__ = ("eng", "fn", "waits", "signal", "token", "is_dma", "idx", "drain")

    def __init__(self, eng, fn, is_dma):
        self.eng = eng
        self.fn = fn
        self.waits = []
        self.signal = False
        self.token = None
        self.is_dma = is_dma
        self.idx = -1
        self.drain = None


class Sched:
    ENGS = ("pe", "act", "dve", "pool", "sp")
    DMAQ = ("sp", "pool", "act")
    NDMA_SEM = 20

    def __init__(self, nc):
        self.nc = nc
        self.ops = {e: [] for e in self.ENGS}
        self.nops = 0
        self.bar_scr = None
        self.bar_ps = None
        self.bar_res = None
        self.scr_res = None

    def _tiny(self, engname, e):
        scr = self.bar_scr
        if engname == "pe":
            return e.matmul(self.bar_ps[0:1, 0:1], lhsT=scr[0:1, 8:9], rhs=scr[0:1, 9:10], start=True, stop=True)
        if engname == "act":
            return e.activation(out=scr[0:1, 0:1], in_=scr[0:1, 10:11], func=AF.Copy)
        if engname == "dve":
            return e.memset(scr[0:1, 2:3], 0.0)
        if engname == "pool":
            return e.memset(scr[0:1, 4:5], 0.0)
        return e.dma_start(out=scr[0:1, 6:7], in_=scr[0:1, 11:12])

    def _deps(self, op, reads, writes):
        ex = [r for r in reads if r.excl and r not in writes]
        if ex:
            writes = list(writes) + ex
        deps = []
        for r in reads:
            if r.writer is not None:
                deps.append((r.writer, "raw"))
        for w in writes:
            if w.writer is not None:
                deps.append((w.writer, "waw"))
            for rd in w.readers:
                deps.append((rd, "war"))
        seen = set()
        for p, kind in deps:
            if p is op or id(p) in seen:
                continue
            if p.eng == op.eng and not p.is_dma and not op.is_dma:
                if op.eng == "pe":
                    continue
                if kind == "war" and not STRICT_WAR:
                    continue
            seen.add(id(p))
            op.waits.append(p)
            p.signal = True
        for r in reads:
            r.readers.append(op)
        for w in writes:
            w.writer = op
            w.readers = []

    def op(self, eng, fn, reads=(), writes=()):
        o = Op(eng, fn, False)
        o.idx = self.nops
        self.nops += 1
        self._deps(o, reads, writes)
        self.ops[eng].append(o)
        return o

    def dma(self, eng, out, in_, reads=(), writes=(), **kw):
        o = Op(eng, lambda e: e.dma_start(out=out, in_=in_, **kw), True)
        o.idx = self.nops
        self.nops += 1
        self._deps(o, reads, writes)
        self.ops[eng].append(o)
        return o

    def barrier(self):
        pre = []
        for e in self.ENGS:
            o = Op(e, ("bar_pre", e), False)
            o.idx = self.nops
            self.nops += 1
            if not hasattr(self, "tiny_res"):
                self.tiny_res = {x: Res("tiny_" + x) for x in self.ENGS}
            self._deps(o, [self.scr_res] if self.scr_res is not None else [],
                       [self.tiny_res[e]] + ([self.bar_res] if (e == "pe" and self.bar_res is not None) else []))
            o.signal = True
            self.ops[e].append(o)
            pre.append(o)
        for e in self.ENGS:
            o = Op(e, ("bar_post", e), False)
            o.idx = self.nops
            self.nops += 1
            o.waits = [p for p in pre if p.eng != e]
            self.ops[e].append(o)

    def emit(self, final_ops=()):
        nc = self.nc
        with contextlib.ExitStack() as es:
            esem = {e: es.enter_context(nc.semaphore("s_" + e)) for e in self.ENGS}
            dsem = {e: [es.enter_context(nc.semaphore("d_%s_%d" % (e, i))) for i in range(self.NDMA_SEM)]
                    for e in self.DMAQ}
            block = es.enter_context(nc.Block())
            ecount = {e: 0 for e in self.ENGS}
            dcount = {e: [0] * self.NDMA_SEM for e in dsem}
            dnext = {e: 0 for e in dsem}
            allops = sorted([o for e in self.ENGS for o in self.ops[e]], key=lambda o: o.idx)
            pre_wait = {}
            for o in allops:
                if o.is_dma:
                    k = dnext[o.eng] % self.NDMA_SEM
                    dnext[o.eng] += 1
                    if dcount[o.eng][k] > 0:
                        pre_wait[id(o)] = (dsem[o.eng][k], 16 * dcount[o.eng][k])
                    dcount[o.eng][k] += 1
                    o.token = (dsem[o.eng][k], 16 * dcount[o.eng][k])
                elif o.signal:
                    ecount[o.eng] += (16 if o.eng == "sp" else 1)
                    o.token = (esem[o.eng], ecount[o.eng])
                    if isinstance(o.fn, tuple) and o.eng in dsem:
                        o.drain = [(dsem[o.eng][k2], 16 * dcount[o.eng][k2])
                                   for k2 in range(self.NDMA_SEM) if dcount[o.eng][k2] > 0]
            final_tokens = [o.token for o in final_ops]

            def run(engname, e):
                waited = {}

                def wait(sem, val):
                    if waited.get(id(sem), 0) >= val:
                        return
                    waited[id(sem)] = val
                    e.wait_ge(sem, val)

                for o in self.ops[engname]:
                    for p in o.waits:
                        wait(*p.token)
                    if id(o) in pre_wait:
                        wait(*pre_wait[id(o)])
                    if isinstance(o.fn, tuple):
                        if o.fn[0] == "bar_pre":
                            for sem2, val2 in (o.drain or ()):
                                wait(sem2, val2)
                            ins = self._tiny(engname, e)
                            ins.then_inc(o.token[0], 16 if engname == "sp" else 1)
                        continue
                    ins = o.fn(e)
                    if o.is_dma:
                        ins.then_inc(o.token[0], 16)
                    elif o.signal:
                        ins.then_inc(o.token[0], 1)
                if engname == "sp":
                    for sem, val in final_tokens:
                        wait(sem, val)

            @block.tensor
            def _(e):
                run("pe", e)

            @block.scalar
            def _(e):
                run("act", e)

            @block.vector
            def _(e):
                run("dve", e)

            @block.gpsimd
            def _(e):
                run("pool", e)

            @block.sync
            def _(e):
                run("sp", e)


class Arena:
    def __init__(self, ap):
        self.ap = ap
        self.off = 0

    def reset(self):
        self.off = 0

    def f32(self, n, name=""):
        a = self.ap[:, self.off:self.off + n]
        self.off += n
        assert self.off <= ARENA_WORDS, ("arena overflow", name, self.off)
        return a, Res(name)

    def bf16(self, n, name=""):
        w = (n + 1) // 2
        a = self.ap[:, self.off:self.off + w].bitcast(BF16)[:, 0:n]
        self.off += w
        assert self.off <= ARENA_WORDS, ("arena overflow", name, self.off)
        return a, Res(name)


def ACT(out, in_, func, scale=None, bias=None, accum=None):
    kw = {}
    if scale is not None:
        kw["scale"] = scale
    if bias is not None:
        kw["bias"] = bias
    if accum is not None:
        kw["accum_out"] = accum
    return lambda e: e.activation(out=out, in_=in_, func=func, **kw)


def TT(out, in0, in1, op):
    return lambda e: e.tensor_tensor(out=out, in0=in0, in1=in1, op=op)


def TS(out, in0, s1, op0, s2=None, op1=None):
    if op1 is None:
        return lambda e: e.tensor_scalar(out=out, in0=in0, scalar1=s1, scalar2=None, op0=op0)
    return lambda e: e.tensor_scalar(out=out, in0=in0, scalar1=s1, scalar2=s2, op0=op0, op1=op1)


def STT(out, in0, scalar, in1, op0, op1):
    return lambda e: e.scalar_tensor_tensor(out=out, in0=in0, scalar=scalar, in1=in1, op0=op0, op1=op1)


def CP(out, in_):
    return lambda e: e.tensor_copy(out=out, in_=in_)


def MM(out, lhsT, rhs, start, stop):
    return lambda e: e.matmul(out, lhsT=lhsT, rhs=rhs, start=start, stop=stop)


def TR(out, in_, ident):
    return lambda e: e.transpose(out, in_, ident)


def MSET(ap, v):
    return lambda e: e.memset(ap, v)


def build(nlayers=4):
    nc = bass.Bass("TRN2", target_bir_lowering=False)

    def din(name, shape, dt=F32):
        return nc.dram_tensor(name, list(shape), dt, kind="ExternalInput").ap()

    def dscr(name, shape, dt=F32):
        return nc.dram_tensor(name, list(shape), dt, kind="Internal").ap()

    x_d = din("x", [4096, D])
    ctx_d = din("ctx", [NCTX, D])
    ccT_d = din("ccT", [128, 16])
    ident_d = din("ident16", [128, 128], BF16)
    maskF_d = din("maskF", [128, 128])
    maskB_d = din("maskB", [128, 128])
    cos_d = din("rope_cos", [128, 32 * 128])
    sin_d = din("rope_sin", [128, 32 * 128])
    fnw_d = din("final_norm_w", [D])
    W = []
    for li in range(4):
        kind = KINDS[li]
        p = "l%d_" % li
        w = dict(norm_w=din(p + "norm_w", [D]), ada_w=din(p + "ada_w", [D, 3 * D]), ada_b=din(p + "ada_b", [3 * D]),
                 in_w=din(p + "in_w", [D, NIN[kind]]), hnwT=din(p + "hnwT", [128, 16]), out_w=din(p + "out_w", [DI, D]))
        if kind == "mlstm":
            w["convw"] = din(p + "convw", [128, 16 * 9])
            w["convb"] = din(p + "convb", [128, 16])
            w["gate_b"] = din(p + "gate_b", [16])
        elif kind == "gla":
            w["w2"] = din(p + "w2", [16, 2 * 512])
            w["gkb"] = din(p + "gkb", [128, 8])
        else:
            w["decay"] = din(p + "decay", [8])
        W.append(w)
    out_d = nc.dram_tensor("out", [4096, D], F32, kind="ExternalOutput").ap()

    h_d = dscr("h_s", [LTOK, D])
    ada_d = dscr("ada_s", [4, 2, 3 * D])
    qT_d = [dscr("qT_s%d" % i, [1024, LTOK], BF16) for i in range(2)]
    kT_d = [dscr("kT_s%d" % i, [1024, LTOK], BF16) for i in range(2)]
    ktm_d = [dscr("ktm_s%d" % i, [LTOK, 1024], BF16) for i in range(2)]
    v_d = dscr("v_s", [LTOK, DI], BF16)
    z_d = dscr("z_s", [LTOK, DI], BF16)
    gates_d = dscr("gates_s", [LTOK, 16])
    ofw_d = dscr("ofw_s", [LTOK, DI])
    qkraw_d = dscr("qkraw_s", [1024, LTOK])
    code_d = dscr("code_s", [32, LTOK])

    with contextlib.ExitStack() as es:
        def sb(name, shape, dt=F32):
            return es.enter_context(nc.sbuf_tensor("sb_" + name, list(shape), dt))

        arena_t = sb("arena", [128, ARENA_WORDS])
        bscr = sb("bscr", [1, 16], BF16)
        ident = sb("ident", [128, 128], BF16)
        maskF = sb("maskF", [128, 128])
        maskB = sb("maskB", [128, 128])
        onesF = sb("onesF", [128, 128])
        ones16 = sb("ones16", [128, 2], BF16)
        ccol = sb("ccol", [128, 8])
        gdec = sb("gdec", [128, 2 * 4 * NT])
        ps_all = es.enter_context(nc.psum_tensor("ps_all", [128, 4096], F32))
        PB = [ps_all[:, i * 512:(i + 1) * 512] for i in range(8)]
        PBR = [Res("pb%d" % i, excl=True) for i in range(8)]
        PB16 = [PB[i].bitcast(BF16) for i in range(8)]

        S = Sched(nc)
        S.bar_scr = bscr
        S.bar_ps = PB[7]
        S.bar_res = PBR[7]
        S.scr_res = Res("bscr")
        A = Arena(arena_t[:])
        Rc = Res("consts")

        S.op("dve", MSET(bscr[:], 0.0), writes=[S.scr_res])
        S.dma("sp", ident[:], ident_d[:, :], writes=[Rc])
        S.dma("sp", maskF[:], maskF_d[:, :], writes=[Rc])
        S.dma("sp", maskB[:], maskB_d[:, :], writes=[Rc])
        S.op("pool", MSET(onesF[:], 1.0), writes=[Rc])
        S.op("pool", MSET(ones16[:], 1.0), writes=[Rc])
        S.op("pool", MSET(ccol[:, 0:1], EPS), writes=[Rc])
        S.op("pool", MSET(ccol[:, 1:2], 1.0), writes=[Rc])
        S.op("pool", MSET(ccol[:, 2:3], -math.log(16.0)), writes=[Rc])
        S.op("pool", MSET(ccol[:, 3:4], math.log(128.0 ** -0.5)), writes=[Rc])
        eps_c, one_c, nl16_c, lqs_c = ccol[:, 0:1], ccol[:, 1:2], ccol[:, 2:3], ccol[:, 3:4]

        A.reset()
        ccT, Rcc = A.f32(16, "ccT")
        scT, Rsc = A.f32(16, "scT")
        adab, Radab = A.f32(3 * D, "adab")
        adasb, Radasb = A.f32(3 * D, "adasb")
        wst2 = [A.f32(8 * 512, "wst%d" % i) for i in range(2)]
        S.dma("sp", ccT, ccT_d[:, :], writes=[Rcc])
        S.op("act", ACT(scT, ccT, AF.Silu), reads=[Rcc], writes=[Rsc])
        for li in range(nlayers):
            S.dma("sp", adab[0:2, :], W[li]["ada_b"].partition_broadcast(2), writes=[Radab])
            aw = W[li]["ada_w"].rearrange("(t p) n -> p t n", p=128)
            for nb in range(6):
                wst, Rw = wst2[nb % 2]
                wst3 = wst.rearrange("p (t n) -> p t n", n=512)
                S.dma("sp", wst3, aw[:, :, nb * 512:(nb + 1) * 512], writes=[Rw])
                pb = nb % 2
                for kt in range(8):
                    S.op("pe", MM(PB[pb][0:2, :], scT[:, 2 * kt:2 * kt + 2], wst3[:, kt, :], kt == 0, kt == 7),
                         reads=[Rsc, Rw], writes=[PBR[pb]])
                S.op("dve", TT(adasb[0:2, nb * 512:(nb + 1) * 512], PB[pb][0:2, :], adab[0:2, nb * 512:(nb + 1) * 512], ALU.add),
                     reads=[PBR[pb], Radab], writes=[Radasb])
            S.dma("sp", ada_d[li], adasb[0:2, :], reads=[Radasb])
        S.barrier()

        final_ops = []
        for li in range(nlayers):
            kind = KINDS[li]
            Wl = W[li]
            last = (li == nlayers - 1)

            def h_src(tt, li=li):
                if li > 0:
                    return h_d[tt * 128:(tt + 1) * 128, :]
                if tt < 2:
                    return ctx_d[tt * 128:(tt + 1) * 128, :]
                return x_d[(tt - 2) * 128:(tt - 1) * 128, :]
            scalar_mix = kind in ("mlstm", "ret")
            A.reset()
            uT, RuT = A.bf16(8 * LTOK, "uT")
            uT3 = uT.rearrange("p (k t) -> p k t", t=LTOK)
            hin = [A.f32(D, "hin%d" % i) for i in range(2)]
            utm = [A.bf16(D, "utm%d" % i) for i in range(2)]
            tmp32, Rtmp = A.f32(D, "tmp32")
            sq, Rsq = A.f32(D, "sq")
            nwbc, Rnw = A.f32(D, "nwbc")
            scbc, Rscb = A.f32(D, "scbc")
            shbc, Rshb = A.f32(D, "shbc")
            Abc, RA = A.f32(D, "Abc")
            ss, Rss = A.f32(NT, "ss")
            rs, Rrs = A.f32(NT, "rs")

            S.dma("sp", nwbc, Wl["norm_w"].partition_broadcast(128), writes=[Rnw])
            Rsst = [Res() for _ in range(NT)]

            def p1_stats(tt):
                hb, Rh = hin[tt % 2]
                S.dma("sp", hb, h_src(tt), writes=[Rh])
                S.op("act", ACT(sq, hb, AF.Square, accum=ss[:, tt:tt + 1]), reads=[Rh], writes=[Rsq, Rsst[tt]])
                S.op("act", ACT(rs[:, tt:tt + 1], ss[:, tt:tt + 1], AF.Ln, scale=1.0 / D, bias=eps_c), reads=[Rsst[tt], Rc], writes=[Rsst[tt]])
                S.op("act", ACT(rs[:, tt:tt + 1], rs[:, tt:tt + 1], AF.Exp, scale=-0.5), reads=[Rsst[tt]], writes=[Rsst[tt]])

            for tt in range(NT):
                if tt == 0 or tt == 2:
                    which = 1 if tt == 0 else 0
                    S.dma("sp", scbc, ada_d[li, which, D:2 * D].partition_broadcast(128), writes=[Rscb])
                    S.dma("sp", shbc, ada_d[li, which, 0:D].partition_broadcast(128), writes=[Rshb])
                    S.op("dve", STT(Abc, scbc, 1.0, nwbc, ALU.add, ALU.mult), reads=[Rscb, Rnw], writes=[RA])
                hb, Rh = hin[tt % 2]
                ub, Ru = utm[tt % 2]
                if tt == 0:
                    p1_stats(0)
                if tt + 1 < NT:
                    p1_stats(tt + 1)
                S.op("dve", STT(tmp32, hb, rs[:, tt:tt + 1], Abc, ALU.mult, ALU.mult), reads=[Rh, Rsst[tt], RA, Rsq], writes=[Rtmp])
                S.op("dve", TT(ub, tmp32, shbc, ALU.add), reads=[Rtmp, Rshb], writes=[Ru])
                pb = tt % 2
                for kt in range(8):
                    S.op("pe", TR(PB16[pb][:, kt * 128:(kt + 1) * 128], ub[:, kt * 128:(kt + 1) * 128], ident[:]),
                         reads=[Ru, Rc], writes=[PBR[pb]])
                S.op("act", CP_ACT(uT3[:, :, tt * 128:(tt + 1) * 128], PB16[pb].rearrange("p (k t) -> p k t", t=128)),
                     reads=[PBR[pb]], writes=[RuT])
            S.barrier()
            A.reset()
            uT, RuT = A.bf16(8 * LTOK, "uT")
            uT3 = uT.rearrange("p (k t) -> p k t", t=LTOK)
            wstd = [A.f32(8 * 512, "wst%d" % i) for i in range(2)]
            w16 = [A.bf16(8 * 512, "w16_%d" % i) for i in range(2)]
            ev = [A.bf16(512, "ev%d" % i) for i in range(3)]
            ev32 = [A.f32(16, "ev32_%d" % i) for i in range(2)]
            tstage = [A.bf16(8 * 128, "tstage%d" % i) for i in range(2)]
            wcount = [0]
            in_w3 = Wl["in_w"].rearrange("(t p) n -> p t n", p=128)

            if kind == "mlstm":
                wgroups = [(2048 + g * 512, 512) for g in range(8)] + [(6144, 16)] + [(g * 512, 512) for g in range(4)]
            elif kind == "ret":
                wgroups = [(2048 + g * 512, 512) for g in range(8)] + [(g * 512, 512) for g in range(4)]
            else:
                wgroups = [(1024 + g * 512, 512) for g in range(8)] + [(5120, 32)] + [(g * 512, 512) for g in range(2)]
            wloaded = {}

            def ensure_w(i):
                if i >= len(wgroups) or i in wloaded:
                    return
                c0, n = wgroups[i]
                wb, Rwb = w16[i % 2]
                wsb, Rwsb = wstd[i % 2]
                wb3 = wb.rearrange("p (t n) -> p t n", n=512)
                ws3 = wsb.rearrange("p (t n) -> p t n", n=512)
                S.dma("sp", ws3[:, :, 0:n], in_w3[:, :, c0:c0 + n], writes=[Rwsb])
                S.op("pool", CP(wb3[:, :, 0:n], ws3[:, :, 0:n]), reads=[Rwsb], writes=[Rwb])
                wloaded[i] = (wb3, Rwb)

            def load_w(c0, n):
                i = wcount[0]
                wcount[0] += 1
                assert wgroups[i] == (c0, n), (wgroups[i], c0, n)
                ensure_w(i)
                r = wloaded[i]
                if int(os.environ.get("WPREF", "1")):
                    ensure_w(i + 1)
                return r

            mrot = [0]

            def tm_group(c0, n, sink):
                wb3, Rwb = load_w(c0, n)
                for tt in range(NT):
                    pb = 2 + mrot[0] % 3
                    mrot[0] += 1
                    for kt in range(8):
                        S.op("pe", MM(PB[pb][:, 0:n], uT3[:, kt, tt * 128:(tt + 1) * 128], wb3[:, kt, 0:n], kt == 0, kt == 7),
                             reads=[RuT, Rwb], writes=[PBR[pb]])
                    sink(tt, PB[pb][:, 0:n], PBR[pb])

            evrot = [0]

            def sink_store(dst, dc0, n, dt_bf16=True):
                def f(tt, ps, Rps):
                    i = evrot[0] % (3 if dt_bf16 else 2)
                    evrot[0] += 1
                    eb, Re = ev[i] if dt_bf16 else ev32[i]
                    eng = "act" if (evrot[0] % 2) else "dve"
                    if eng == "act":
                        S.op("act", ACT(eb[:, 0:n], ps, AF.Copy), reads=[Rps], writes=[Re])
                    else:
                        S.op("dve", CP(eb[:, 0:n], ps), reads=[Rps], writes=[Re])
                    S.dma("sp", dst[tt * 128:(tt + 1) * 128, dc0:dc0 + n], eb[:, 0:n], reads=[Re])
                return f

            def sink_silu(dst, dc0, n):
                def f(tt, ps, Rps):
                    i = evrot[0] % 3
                    evrot[0] += 1
                    eb, Re = ev[i]
                    S.op("act", ACT(eb[:, 0:n], ps, AF.Silu), reads=[Rps], writes=[Re])
                    S.dma("sp", dst[tt * 128:(tt + 1) * 128, dc0:dc0 + n], eb[:, 0:n], reads=[Re])
                return f

            def fm_tile(wb3, Rwb, j, sink, m=128):
                for tb in range(9):
                    t0 = tb * 512
                    n = min(512, LTOK - t0)
                    pb = 2 + mrot[0] % 3
                    mrot[0] += 1
                    for kt in range(8):
                        S.op("pe", MM(PB[pb][0:m, 0:n], wb3[:, kt, j * 128:j * 128 + m], uT3[:, kt, t0:t0 + n], kt == 0, kt == 7),
                             reads=[RuT, Rwb], writes=[PBR[pb]])
                    sink(t0, n, PB[pb][0:m, 0:n], PBR[pb])

            trot = [0]

            def k_to_tm(src, Rsrc, dst_d, dcol):
                dst3 = dst_d.rearrange("(t p) d -> p t d", p=128)
                for g0 in range(0, NT, 8):
                    ng = min(8, NT - g0)
                    pb = trot[0] % 2
                    stg, Rst = tstage[trot[0] % 2]
                    trot[0] += 1
                    for i in range(ng):
                        S.op("pe", TR(PB16[pb][:, i * 128:(i + 1) * 128], src[:, (g0 + i) * 128:(g0 + i + 1) * 128], ident[:]),
                             reads=[Rsrc, Rc], writes=[PBR[pb]])
                    S.op("act", ACT(stg[:, 0:ng * 128], PB16[pb][:, 0:ng * 128], AF.Copy), reads=[PBR[pb]], writes=[Rst])
                    S.dma("sp", dst3[:, g0:g0 + ng, dcol:dcol + 128], stg[:, 0:ng * 128].rearrange("p (t d) -> p t d", d=128),
                          reads=[Rst])

            if kind == "mlstm":
                w16x = w16 + [A.bf16(8 * 512, "w16_2")]
                owner = [None, None, None]
                stg_i = [0]
                cur_qk, cur_tm, pref_tm = [None], [None], [None]

                def load_group(c0, n, keep):
                    key = (c0, n)
                    for i in range(3):
                        if owner[i] == key:
                            break
                    else:
                        for i in range(3):
                            if owner[i] is None or owner[i] not in keep:
                                break
                        else:
                            raise RuntimeError("no free weight buffer")
                        owner[i] = key
                        wsb, Rwsb = wstd[stg_i[0] % 2]
                        stg_i[0] += 1
                        ws3 = wsb.rearrange("p (t n) -> p t n", n=512)
                        wbx = w16x[i][0].rearrange("p (t n) -> p t n", n=512)
                        S.dma("sp", ws3[:, :, 0:n], in_w3[:, :, c0:c0 + n], writes=[Rwsb])
                        S.op("pool", CP(wbx[:, :, 0:n], ws3[:, :, 0:n]), reads=[Rwsb], writes=[w16x[i][1]])
                    return w16x[i][0].rearrange("p (t n) -> p t n", n=512), w16x[i][1]

                def sink_act(dst, dc0, n, func, f32=False):
                    def f(tt, ps, Rps):
                        i = evrot[0] % (2 if f32 else 3)
                        evrot[0] += 1
                        eb, Re = ev32[i] if f32 else ev[i]
                        S.op("act", ACT(eb[:, 0:n], ps, func), reads=[Rps], writes=[Re])
                        S.dma("sp", dst[tt * 128:(tt + 1) * 128, dc0:dc0 + n], eb[:, 0:n], reads=[Re])
                    return f

                tm_list = ([(2048 + g * 512, 512, sink_act(v_d, g * 512, 512, AF.Copy)) for g in range(4)]
                           + [(4096 + g * 512, 512, sink_act(z_d, g * 512, 512, AF.Silu)) for g in range(4)]
                           + [(6144, 16, sink_act(gates_d, 0, 16, AF.Copy, f32=True))])

                def tm_stream():
                    for j, (c0, n, sink) in enumerate(tm_list):
                        cur_tm[0] = (c0, n)
                        pref_tm[0] = None
                        wb3_, Rwb_ = load_group(c0, n, keep={cur_qk[0]})
                        for tt in range(NT):
                            if tt == 8 and j + 1 < len(tm_list):
                                nc0, nn, _ = tm_list[j + 1]
                                pref_tm[0] = (nc0, nn)
                                load_group(nc0, nn, keep={cur_qk[0], cur_tm[0]})
                            pb = 2 + mrot[0] % 3
                            mrot[0] += 1
                            for kt in range(8):
                                S.op("pe", MM(PB[pb][:, 0:n], uT3[:, kt, tt * 128:(tt + 1) * 128], wb3_[:, kt, 0:n], kt == 0, kt == 7),
                                     reads=[RuT, Rwb_], writes=[PBR[pb]])
                            sink(tt, PB[pb][:, 0:n], PBR[pb])
                            yield

                tms = tm_stream()

                def pull(k):
                    for _ in range(k):
                        if next(tms, "end") == "end":
                            break
                preb = [A.f32(LTOK, "pre%d" % i) for i in range(2)]
                acc, Racc = A.f32(LTOK, "acc")
                outb = [A.bf16(LTOK, "outb%d" % i) for i in range(2)]
                cw, Rcw = A.f32(16 * 9, "cw")
                cb, Rcb = A.f32(16, "cb")
                S.dma("sp", cw, Wl["convw"][:, :], writes=[Rcw])
                S.dma("sp", cb, Wl["convb"][:, :], writes=[Rcb])
                acc3 = acc[:, NCTX:].rearrange("p (r c) -> p r c", c=64)
                wcur = {}

                def do_fm(dt):
                    g, j = divmod(dt, 4)
                    cur_qk[0] = (g * 512, 512)
                    wb3, Rwb = load_group(g * 512, 512, keep={cur_tm[0], pref_tm[0]})
                    pbuf, Rpbuf = preb[dt % 2]

                    def sink_pre(t0, n, ps, Rps):
                        S.op("act", ACT(pbuf[:, t0:t0 + n], ps, AF.Copy), reads=[Rps], writes=[Rpbuf])
                    fm_tile(wb3, Rwb, j, sink_pre)

                do_fm(0)
                for dt in range(16):
                    if dt + 1 < 16:
                        do_fm(dt + 1)
                    pre, Rpre = preb[dt % 2]
                    pre3 = pre[:, NCTX:].rearrange("p (r c) -> p r c", c=64)
                    tap = lambda a, b: cw[:, dt * 9 + a * 3 + b: dt * 9 + a * 3 + b + 1]
                    S.op("dve", TS(acc, pre, tap(1, 1), ALU.mult, cb[:, dt:dt + 1], ALU.add), reads=[Rpre, Rcw, Rcb], writes=[Racc])
                    S.op("dve", STT(acc[:, 1:NCTX], pre[:, 0:NCTX - 1], tap(1, 0), acc[:, 1:NCTX], ALU.mult, ALU.add),
                         reads=[Rpre, Racc], writes=[Racc])
                    S.op("dve", STT(acc[:, 0:NCTX - 1], pre[:, 1:NCTX], tap(1, 2), acc[:, 0:NCTX - 1], ALU.mult, ALU.add),
                         reads=[Rpre, Racc], writes=[Racc])
                    for a in range(3):
                        for b in range(3):
                            if a == 1 and b == 1:
                                continue
                            dr, dc = a - 1, b - 1
                            r0, r1 = max(0, -dr), 64 - max(0, dr)
                            c0, c1 = max(0, -dc), 64 - max(0, dc)
                            S.op("dve", STT(acc3[:, r0:r1, c0:c1], pre3[:, r0 + dr:r1 + dr, c0 + dc:c1 + dc], tap(a, b),
                                            acc3[:, r0:r1, c0:c1], ALU.mult, ALU.add), reads=[Rpre, Racc], writes=[Racc])
                    pull(20)
                    ob, Rob = outb[dt % 2]
                    S.op("act", ACT(ob, acc, AF.Silu), reads=[Racc], writes=[Rob])
                    if dt < 8:
                        S.dma("sp", qT_d[0][dt * 128:(dt + 1) * 128, :], ob, reads=[Rob])
                    else:
                        S.dma("sp", kT_d[0][(dt - 8) * 128:(dt - 7) * 128, :], ob, reads=[Rob])
                        k_to_tm(ob, Rob, ktm_d[0], (dt - 8) * 128)
                pull(10000)
            elif kind == "ret":
                for g in range(4):
                    tm_group(2048 + g * 512, 512, sink_store(v_d, g * 512, 512))
                for g in range(4):
                    tm_group(4096 + g * 512, 512, sink_silu(z_d, g * 512, 512))
                cosT, Rcos = A.f32(32 * 128, "cos")
                sinT, Rsin = A.f32(32 * 128, "sin")
                S.dma("sp", cosT, cos_d[:, :], writes=[Rcos])
                S.dma("sp", sinT, sin_d[:, :], writes=[Rsin])
                rsb = [A.f32(512, "rsb%d" % i) for i in range(2)]
                rt = [A.f32(256, "rt%d" % i) for i in range(4)]
                rot = [A.bf16(512, "rot%d" % i) for i in range(2)]
                rcount = [0]
                rpend = [None]
                for g in range(4):
                    isq = g < 2

                    def sink_rope(tt, ps, Rps, g=g, isq=isq):
                        i = rcount[0] % 2
                        rcount[0] += 1
                        rb, Rrb = rot[i]
                        if tt < 2:
                            S.op("act", ACT(rb, ps, AF.Copy), reads=[Rps], writes=[Rrb])
                        else:
                            sbf, Rsb = rsb[i]
                            S.op("act", ACT(sbf, ps, AF.Copy), reads=[Rps], writes=[Rsb])
                            k = tt - 2
                            sv = sbf.rearrange("p (h s w i) -> p h s w i", h=2, s=2, w=2)
                            rv = rb.rearrange("p (h s w i) -> p h s w i", h=2, s=2, w=2)
                            u1, u2 = sv[:, :, :, 0, :], sv[:, :, :, 1, :]
                            cs = cosT[:, k * 128:(k + 1) * 128].rearrange("p (s i) -> p s i", i=64).unsqueeze(1).to_broadcast([128, 2, 2, 64])
                            sn = sinT[:, k * 128:(k + 1) * 128].rearrange("p (s i) -> p s i", i=64).unsqueeze(1).to_broadcast([128, 2, 2, 64])
                            t = [rt[j][0].rearrange("p (h s i) -> p h s i", h=2, s=2) for j in range(4)]
                            Rt = [rt[j][1] for j in range(4)]
                            S.op("dve", TT(t[0], u1, cs, ALU.mult), reads=[Rsb, Rcos], writes=[Rt[0]])
                            S.op("pool", TT(t[1], u2, sn, ALU.mult), reads=[Rsb, Rsin], writes=[Rt[1]])
                            S.op("dve", TT(rv[:, :, :, 0, :], t[0], t[1], ALU.subtract), reads=[Rt[0], Rt[1]], writes=[Rrb])
                            S.op("pool", TT(t[2], u1, sn, ALU.mult), reads=[Rsb, Rsin], writes=[Rt[2]])
                            S.op("dve", TT(t[3], u2, cs, ALU.mult), reads=[Rsb, Rcos], writes=[Rt[3]])
                            S.op("dve", TT(rv[:, :, :, 1, :], t[2], t[3], ALU.add), reads=[Rt[2], Rt[3]], writes=[Rrb])
                        prev = rpend[0]
                        rpend[0] = lambda rb=rb, Rrb=Rrb, tt=tt: rope_post(rb, Rrb, tt, g, isq)
                        if prev is not None:
                            prev()

                    def rope_post(rb, Rrb, tt, g, isq):
                        pb = trot[0] % 2
                        stg, Rst = tstage[trot[0] % 2]
                        trot[0] += 1
                        for j in range(4):
                            S.op("pe", TR(PB16[pb][:, j * 128:(j + 1) * 128], rb[:, j * 128:(j + 1) * 128], ident[:]),
                                 reads=[Rrb, Rc], writes=[PBR[pb]])
                        S.op("act", ACT(stg[:, 0:512], PB16[pb][:, 0:512], AF.Copy), reads=[PBR[pb]], writes=[Rst])
                        dstT = (qT_d[0] if isq else kT_d[0]).rearrange("(a p) t -> p a t", p=128)
                        a0 = (g % 2) * 4
                        S.dma("sp", dstT[:, a0:a0 + 4, tt * 128:(tt + 1) * 128], stg[:, 0:512].rearrange("p (a t) -> p a t", t=128),
                              reads=[Rst])
                        if not isq:
                            S.dma("sp", ktm_d[0][tt * 128:(tt + 1) * 128, (g - 2) * 512:(g - 1) * 512], rb, reads=[Rrb])
                    tm_group(g * 512, 512, sink_rope)
                    if rpend[0] is not None:
                        rpend[0]()
                        rpend[0] = None
            else:
                for g in range(4):
                    tm_group(1024 + g * 512, 512, sink_store(v_d, g * 512, 512))
                for g in range(4):
                    tm_group(3072 + g * 512, 512, sink_silu(z_d, g * 512, 512))
                fst = [A.f32(512, "fst%d" % i) for i in range(2)]
                frot = [0]

                def sink_fm_store(dst_rows):
                    def f(t0, n, ps, Rps):
                        i = frot[0] % 2
                        frot[0] += 1
                        fb, Rfb = fst[i]
                        m = dst_rows.shape[0]
                        S.op("act", ACT(fb[0:m, 0:n], ps, AF.Copy), reads=[Rps], writes=[Rfb])
                        S.dma("sp", dst_rows[:, t0:t0 + n], fb[0:m, 0:n], reads=[Rfb])
                    return f
                wb3, Rwb = load_w(5120, 32)
                fm_tile(wb3, Rwb, 0, sink_fm_store(code_d[0:32, :]), m=32)
                for g in range(2):
                    wb3, Rwb = load_w(g * 512, 512)
                    for j in range(4):
                        r0 = (g * 4 + j) * 128
                        fm_tile(wb3, Rwb, j, sink_fm_store(qkraw_d[r0:r0 + 128, :]))
                S.barrier()
                A.reset()
                qf, Rqf = A.f32(LTOK, "qf")
                kf, Rkf = A.f32(LTOK, "kf")
                l1, Rl1 = A.f32(LTOK, "l1")
                Lc, RLc = A.f32(LTOK, "Lc")
                Dt, RDt = A.f32(LTOK, "Dt")
                Et, REt = A.f32(LTOK, "Et")
                codeT = [A.f32(LTOK, "code%d" % i) for i in range(2)]
                go = [A.bf16(LTOK, "go%d" % i) for i in range(3)]
                w2sb, Rw2 = A.f32(1024, "w2")
                gkb, Rgkb = A.f32(8, "gkb")
                ngkb, Rngkb = A.f32(8, "ngkb")
                tstage = [A.bf16(8 * 128, "tstage%d" % i) for i in range(2)]
                S.dma("sp", w2sb[0:16, :], Wl["w2"][:, :], writes=[Rw2])
                S.dma("sp", gkb, Wl["gkb"][:, :], writes=[Rgkb])
                S.op("dve", TS(ngkb, gkb, -1.0, ALU.mult), reads=[Rgkb], writes=[Rngkb])
                for dirn in range(2):
                    S.dma("sp", codeT[dirn][0][0:16, :], code_d[dirn * 16:(dirn + 1) * 16, :], writes=[codeT[dirn][1]])
                Lc3 = Lc.rearrange("p (c k) -> p c k", k=128)
                l13 = l1.rearrange("p (c k) -> p c k", k=128)
                Dt3 = Dt.rearrange("p (c k) -> p c k", k=128)
                tot_b = Lc3[:, :, 127:128].to_broadcast([128, NT, 128])
                gdec4 = gdec[:].rearrange("p (d h c) -> p d h c", d=2, h=4)
                for h in range(4):
                    S.dma("sp", qf, qkraw_d[h * 128:(h + 1) * 128, :], writes=[Rqf])
                    S.dma("sp", kf, qkraw_d[512 + h * 128:512 + (h + 1) * 128, :], writes=[Rkf])
                    for dirn in range(2):
                        cT, RcT = codeT[dirn]
                        for tb in range(9):
                            t0 = tb * 512
                            n = min(512, LTOK - t0)
                            pb = 2 + mrot[0] % 3
                            mrot[0] += 1
                            S.op("pe", MM(PB[pb][:, 0:n], w2sb[0:16, dirn * 512 + h * 128: dirn * 512 + (h + 1) * 128], cT[0:16, t0:t0 + n], True, True),
                                 reads=[Rw2, RcT], writes=[PBR[pb]])
                            S.op("act", ACT(l1[:, t0:t0 + n], PB[pb][:, 0:n], AF.Exp, scale=-1.0, bias=ngkb[:, dirn * 4 + h: dirn * 4 + h + 1]),
                                 reads=[PBR[pb], Rngkb], writes=[Rl1])
                        S.op("act", ACT(l1, l1, AF.Ln, bias=one_c), reads=[Rl1, Rc], writes=[Rl1])
                        for c in range(NT):
                            S.op("dve", lambda e, c=c: e.tensor_tensor_scan(out=Lc[:, c * 128:(c + 1) * 128], data0=onesF[:],
                                                                             data1=l1[:, c * 128:(c + 1) * 128], initial=0.0,
                                                                             op0=ALU.mult, op1=ALU.add),
                                 reads=[Rl1, Rc], writes=[RLc])
                        S.op("act", ACT(gdec4[:, dirn, h, :], Lc3[:, :, 127], AF.Exp, scale=-1.0 / 16), reads=[RLc], writes=[Res()])
                        S.op("dve", TT(Dt3, tot_b, Lc3, ALU.subtract), reads=[RLc], writes=[RDt])
                        if dirn == 0:
                            eq, ek, ekh = Lc, Lc, Dt
                            Req, Rek, Rekh = RLc, RLc, RDt
                        else:
                            S.op("dve", TT(Dt, Dt, l1, ALU.add), reads=[RDt, Rl1], writes=[RDt])
                            S.op("dve", TT(Lc, Lc, l1, ALU.subtract), reads=[RLc, Rl1], writes=[RLc])
                            eq, ek, ekh = Dt, Dt, Lc
                            Req, Rek, Rekh = RDt, RDt, RLc
                        oq, Roq = go[0]
                        ok, Rok = go[1]
                        okh, Rokh = go[2]
                        S.op("act", ACT(Et, eq, AF.Exp, scale=-1.0 / 16, bias=lqs_c), reads=[Req, Rc], writes=[REt])
                        S.op("dve", TT(oq, qf, Et, ALU.mult), reads=[Rqf, REt], writes=[Roq])
                        S.dma("sp", qT_d[dirn][h * 128:(h + 1) * 128, :], oq, reads=[Roq])
                        S.op("act", ACT(Et, ek, AF.Exp, scale=1.0 / 16), reads=[Rek, Roq], writes=[REt])
                        S.op("dve", TT(ok, kf, Et, ALU.mult), reads=[Rkf, REt], writes=[Rok])
                        S.dma("sp", kT_d[dirn][h * 128:(h + 1) * 128, :], ok, reads=[Rok])
                        S.op("act", ACT(Et, ekh, AF.Exp, scale=-1.0 / 16), reads=[Rekh, Rok], writes=[REt])
                        S.op("dve", TT(okh, kf, Et, ALU.mult), reads=[Rkf, REt], writes=[Rokh])
                        k_to_tm(okh, Rokh, ktm_d[dirn], h * 128)
            S.barrier()

            if os.environ.get("BACKSKIP"):
                continue
            A.reset()
            ndt = 2 if scalar_mix else 1
            dk = 128 * ndt
            SW = 516
            S32, RS32 = A.f32(4 * 2 * SW, "S32")
            S32v = S32.rearrange("p (h d w) -> p h d w", h=4, d=2)
            S16 = [A.bf16(4 * 2 * SW, "S16_%d" % i) for i in range(2)]
            S16v = [s[0].rearrange("p (h d w) -> p h d w", h=4, d=2) for s in S16]
            RS16 = [[[Res() for _ in range(2)] for _ in range(4)] for _ in range(2)]
            RS32h = [[Res() for _ in range(2)] for _ in range(4)]
            qc = [A.bf16(8 * 128, "qc%d" % i) for i in range(2)]
            kc = [A.bf16(8 * 128, "kc%d" % i) for i in range(2)]
            ktc = [A.bf16(1024, "ktc%d" % i) for i in range(2)]
            vc = [A.bf16(DI, "vc%d" % i) for i in range(2)]
            ku = [A.bf16(1024, "ku%d" % i) for i in range(2)]
            sT16 = [A.bf16(512, "sT%d" % i) for i in range(2)]
            RsTh = [[Res() for _ in range(4)] for _ in range(2)]
            Rkuh = [[Res() for _ in range(4)] for _ in range(2)]
            osb = [A.f32(DI, "osb%d" % i) for i in range(2)]
            Rosbh = [[Res() for _ in range(4)] for _ in range(2)]
            ofwb = [A.f32(DI, "ofw%d" % i) for i in range(2)]
            zcb = [A.bf16(DI, "zc%d" % i) for i in range(2)]
            holdb = [A.f32(D, "hold%d" % i) for i in range(2)]
            szb, Rsz = A.f32(DI, "sz")
            yb, Ryb = A.bf16(DI, "y")
            yT, RyT = A.bf16(16 * 128, "yT")
            hnew, Rhnew = A.f32(D, "hnew")
            ptmp, Rptmp = A.f32(512, "ptmp")
            gbc, Rg = A.f32(D, "gbc")
            hnwbc, Rhnw = A.f32(DI, "hnw")
            ow16, Row = A.bf16(16 * 1024, "ow16")
            ow16v = ow16.rearrange("p (f n) -> p f n", n=1024)
            st8, Rst8 = A.f32(8, "st8")
            sfw, _ = A.f32(NT * 4, "sfw")
            Rsfw = [Res() for _ in range(NT)]
            sbw = [A.f32(4, "sbw%d" % i) for i in range(2)]
            mean4, Rmean = A.f32(4, "mean")
            msq4, Rmsq = A.f32(4, "msq")
            var4, Rvar = A.f32(4, "var")
            rstd4, Rrstd = A.f32(4, "rstd")
            ttmp, Rttmp = A.f32(512, "ttmp")
            fss, Rfss = A.f32(1, "fss")
            frs, Rfrs = A.f32(1, "frs")
            dna, Rdna = A.f32(4, "dna")
            sc2, Rsc2 = A.f32(4, "sc2")
            Rdnah = [Res() for _ in range(4)]
            Rsc2h = [Res() for _ in range(4)]
            NS = NT * 8
            gt, Rgt = A.f32(NT * 16, "gates")
            gbb, Rgbb = A.f32(16, "gate_b")
            nlf, Rnlf = A.f32(NS, "nlf")
            i8, Ri8 = A.f32(NS, "i8")
            cum, Rcum = A.f32(NS, "cum")
            tot, Rtot = A.f32(NS, "tot")
            eB, ReB = A.f32(NS, "eB")
            emB, RemB = A.f32(NS, "emB")
            wsc, Rwsc = A.f32(NS, "wsc")
            usc, Rusc = A.f32(NS, "usc")
            dec, Rdec = A.f32(NS, "dec")
            stmp, Rstmp = A.f32(NS, "stmp")
            owst, Rowst = A.f32(2 * 1024, "owst")
            fnwbc, Rfnw = owst[:, 0:D], Rowst

            ow3 = Wl["out_w"].rearrange("(f p) n -> p f n", p=128)
            owst3 = owst.rearrange("p (f n) -> p f n", n=1024)
            S.dma("sp", hnwbc[:, 0:16], Wl["hnwT"][:, :], writes=[Rhnw])
            for i in range(8):
                S.dma("sp", owst3, ow3[:, 2 * i:2 * i + 2, :], writes=[Rowst])
                for j in range(2):
                    f = 2 * i + j
                    S.op("act", ACT(ow16v[:, f, :], owst3[:, j, :], AF.Copy, scale=hnwbc[:, f:f + 1]), reads=[Rowst, Rhnw], writes=[Row])
            if last:
                S.dma("sp", fnwbc, fnw_d.partition_broadcast(128), writes=[Rfnw])

            if scalar_mix:
                nch = NT if kind == "mlstm" else 1
                nlf3 = nlf.rearrange("p (c g) -> p c g", g=8)
                i83 = i8.rearrange("p (c g) -> p c g", g=8)
                if kind == "mlstm":
                    gt3 = gt.rearrange("p (c g) -> p c g", g=16)
                    gd3 = gates_d.rearrange("(c p) g -> p c g", p=128)
                    S.dma("sp", gt3[:, 0:17, :], gd3[:, 0:17, :], writes=[Rgt])
                    S.dma("sp", gt3[:, 17:NT, :], gd3[:, 17:NT, :], writes=[Rgt])
                    S.dma("sp", gbb, Wl["gate_b"].partition_broadcast(128), writes=[Rgbb])
                    S.op("dve", TT(gt3, gt3, gbb.unsqueeze(1).to_broadcast([128, NT, 16]), ALU.add), reads=[Rgt, Rgbb], writes=[Rgt])
                    for dirn in range(2):
                        S.op("act", ACT(nlf3[:, :, dirn * 4:dirn * 4 + 4], gt3[:, :, dirn * 8 + 4:dirn * 8 + 8], AF.Exp, scale=-1.0),
                             reads=[Rgt], writes=[Rnlf])
                        S.op("dve", CP(i83[:, :, dirn * 4:dirn * 4 + 4], gt3[:, :, dirn * 8:dirn * 8 + 4]), reads=[Rgt], writes=[Ri8])
                    S.op("act", ACT(nlf, nlf, AF.Ln, bias=one_c), reads=[Rnlf, Rc], writes=[Rnlf])
                else:
                    S.dma("sp", gbb[:, 0:8], Wl["decay"].partition_broadcast(128), writes=[Rgbb])
                    S.op("act", ACT(nlf[:, 0:8], gbb[:, 0:8], AF.Exp, scale=-1.0), reads=[Rgbb], writes=[Rnlf])
                    S.op("act", ACT(nlf[:, 0:8], nlf[:, 0:8], AF.Ln, bias=one_c), reads=[Rnlf, Rc], writes=[Rnlf])
                    S.op("dve", MSET(i8[:, 0:8], 0.0), writes=[Ri8])
                for c in range(nch):
                    S.op("pe", MM(PB[0][:, c * 8:c * 8 + 4], maskF[:], nlf[:, c * 8:c * 8 + 4], True, True), reads=[Rnlf, Rc], writes=[PBR[0]])
                    S.op("pe", MM(PB[0][:, c * 8 + 4:c * 8 + 8], maskB[:], nlf[:, c * 8 + 4:c * 8 + 8], True, True), reads=[Rnlf, Rc], writes=[PBR[0]])
                    S.op("pe", MM(PB[1][:, c * 8:c * 8 + 8], onesF[:], nlf[:, c * 8:c * 8 + 8], True, True), reads=[Rnlf, Rc], writes=[PBR[1]])
                n8 = nch * 8
                S.op("dve", CP(cum[:, 0:n8], PB[0][:, 0:n8]), reads=[PBR[0]], writes=[Rcum])
                S.op("dve", CP(tot[:, 0:n8], PB[1][:, 0:n8]), reads=[PBR[1]], writes=[Rtot])
                S.op("act", ACT(eB[:, 0:n8], cum[:, 0:n8], AF.Exp, scale=-1.0), reads=[Rcum], writes=[ReB])
                S.op("act", ACT(emB[:, 0:n8], cum[:, 0:n8], AF.Exp), reads=[Rcum], writes=[RemB])
                S.op("dve", TT(stmp[:, 0:n8], i8[:, 0:n8], cum[:, 0:n8], ALU.add), reads=[Ri8, Rcum], writes=[Rstmp])
                S.op("act", ACT(wsc[:, 0:n8], stmp[:, 0:n8], AF.Exp, bias=nl16_c), reads=[Rstmp, Rc], writes=[Rwsc])
                S.op("dve", TT(stmp[:, 0:n8], stmp[:, 0:n8], tot[:, 0:n8], ALU.subtract), reads=[Rstmp, Rtot, Rwsc], writes=[Rstmp])
                S.op("act", ACT(usc[:, 0:n8], stmp[:, 0:n8], AF.Exp, bias=nl16_c), reads=[Rstmp, Rc], writes=[Rusc])
                S.op("act", ACT(dec[:, 0:n8], tot[:, 0:n8], AF.Exp, scale=-1.0), reads=[Rtot], writes=[Rdec])

            S.barrier()

            def scol(tab, c, dirn, h):
                cc = c if kind == "mlstm" else 0
                o = cc * 8 + dirn * 4 + h
                return tab[:, o:o + 1]

            gdec4 = gdec[:].rearrange("p (d h c) -> p d h c", d=2, h=4)
            nq = 4 * ndt
            Rofw_d = [Res() for _ in range(NT)]
            Rh_d = [Res() for _ in range(NT)]
            PQ = [Res() for _ in range(4)]
            PD = [Res() for _ in range(4)]
            PN = [[Res() for _ in range(2)] for _ in range(4)]
            steps = [(0, c) for c in range(NT)] + [(1, c) for c in [1, 0] + list(range(NT - 1, 1, -1))]
            g_loaded = [None]

            def do_p5(c):
                return not (last and c < 2)

            def issue_loads(g):
                dirn, c = steps[g]
                bi = g % 2
                src = 0 if scalar_mix else dirn
                qsrc = qT_d[src].rearrange("(a p) t -> p a t", p=128)
                ksrc = kT_d[src].rearrange("(a p) t -> p a t", p=128)
                qb3 = qc[bi][0].rearrange("p (a t) -> p a t", t=128)
                kb3 = kc[bi][0].rearrange("p (a t) -> p a t", t=128)
                S.dma("sp", kb3[:, 0:nq, :], ksrc[:, 0:nq, c * 128:(c + 1) * 128], writes=[kc[bi][1]])
                S.dma("sp", qb3[:, 0:nq, :], qsrc[:, 0:nq, c * 128:(c + 1) * 128], writes=[qc[bi][1]])
                S.dma("sp", vc[bi][0], v_d[c * 128:(c + 1) * 128, :], writes=[vc[bi][1]])
                S.dma("sp", ktc[bi][0][:, 0:4 * dk], ktm_d[src][c * 128:(c + 1) * 128, 0:4 * dk], writes=[ktc[bi][1]])

            def issue_p5_loads(g):
                if g >= len(steps):
                    return
                dirn, c = steps[g]
                bi = g % 2
                if dirn == 1 and do_p5(c):
                    S.dma("sp", ofwb[bi][0], ofw_d[c * 128:(c + 1) * 128, :], reads=[Rofw_d[c]], writes=[ofwb[bi][1]])
                    S.dma("sp", zcb[bi][0], z_d[c * 128:(c + 1) * 128, :], writes=[zcb[bi][1]])

            def issue_hold_load(g):
                if not p5_active(g):
                    return
                c = steps[g][1]
                bi = g % 2
                S.dma("sp", holdb[bi][0], h_src(c), reads=[Rh_d[c]], writes=[holdb[bi][1]])

            center = kind != "gla"

            def p5_active(g):
                return 0 <= g < len(steps) and steps[g][0] == 1 and do_p5(steps[g][1])

            def p5_A1(g):
                if not p5_active(g):
                    return
                bi = g % 2
                ob = osb[bi][0]
                ofw, Rofw = ofwb[bi]
                Ro4 = Rosbh[bi]
                for h in range(4):
                    oh = ob[:, h * 512:(h + 1) * 512]
                    S.op("dve", TT(oh, oh, ofw[:, h * 512:(h + 1) * 512], ALU.add), reads=[Ro4[h], Rofw], writes=[Ro4[h]])

            def p5_A1act(g):
                if not p5_active(g):
                    return
                bi = g % 2
                ob = osb[bi][0]
                Ro4 = Rosbh[bi]
                for h in range(4):
                    oh = ob[:, h * 512:(h + 1) * 512]
                    S.op("act", ACT(ttmp, oh, AF.Square, accum=st8[:, 4 + h:5 + h]), reads=[Ro4[h]], writes=[Rttmp, Rst8])

            def p5_mid(g):
                if not p5_active(g):
                    return
                if center:
                    cg = steps[g][1]
                    S.op("dve", TT(mean4, sbw[g % 2][0], sfw[:, cg * 4:cg * 4 + 4], ALU.add), reads=[sbw[g % 2][1], Rsfw[cg]], writes=[Rmean])
                    S.op("dve", TS(mean4, mean4, 1.0 / 512, ALU.mult), reads=[Rmean], writes=[Rmean])
                    S.op("dve", TT(msq4, mean4, mean4, ALU.mult), reads=[Rmean], writes=[Rmsq])
                    S.op("dve", STT(var4, st8[:, 4:8], 1.0 / 512, msq4, ALU.mult, ALU.subtract), reads=[Rst8, Rmsq], writes=[Rvar])
                    S.op("act", ACT(rstd4, var4, AF.Ln, bias=eps_c), reads=[Rvar, Rc], writes=[Rrstd])
                else:
                    S.op("act", ACT(rstd4, st8[:, 4:8], AF.Ln, scale=1.0 / 512, bias=eps_c), reads=[Rst8, Rc], writes=[Rrstd])
                S.op("act", ACT(rstd4, rstd4, AF.Exp, scale=-0.5), reads=[Rrstd], writes=[Rrstd])

            def p5_A2(g):
                if not p5_active(g):
                    return
                bi = g % 2
                ob = osb[bi][0]
                Ro4 = Rosbh[bi]
                szb, Rsz = zcb[bi]
                for h in range(4):
                    oh = ob[:, h * 512:(h + 1) * 512]
                    zh = szb[:, h * 512:(h + 1) * 512]
                    yh = yb[:, h * 512:(h + 1) * 512]
                    if center:
                        S.op("dve", STT(oh, oh, mean4[:, h:h + 1], zh, ALU.subtract, ALU.mult), reads=[Ro4[h], Rmean, Rsz], writes=[Ro4[h]])
                        S.op("act", ACT(yh, oh, AF.Copy, scale=rstd4[:, h:h + 1]), reads=[Ro4[h], Rrstd], writes=[Ryb])
                    else:
                        S.op("dve", STT(yh, oh, rstd4[:, h:h + 1], zh, ALU.mult, ALU.mult), reads=[Ro4[h], Rrstd, Rsz], writes=[Ryb])

            def p5_B1(g):
                if not p5_active(g):
                    return
                for half in range(2):
                    pb = half
                    for i in range(8):
                        f = half * 8 + i
                        S.op("pe", TR(PB16[pb][:, i * 128:(i + 1) * 128], yb[:, f * 128:(f + 1) * 128], ident[:]),
                             reads=[Ryb, Rc], writes=[PBR[pb]])
                    S.op("act", ACT(yT[:, half * 1024:(half + 1) * 1024], PB16[pb], AF.Copy), reads=[PBR[pb]], writes=[RyT])

            def p5_B2(g):
                if not p5_active(g):
                    return
                c = steps[g][1]
                bi = g % 2
                hold, Rhold = holdb[bi]
                want = 1 if c < 2 else 0
                if g_loaded[0] != want:
                    g_loaded[0] = want
                    S.dma("sp", gbc, ada_d[li, want, 2 * D:3 * D].partition_broadcast(128), writes=[Rg])
                yT3 = yT.rearrange("p (f t) -> p f t", t=128)
                for nb in range(2):
                    pb = 4 + nb
                    for f in range(16):
                        S.op("pe", MM(PB[pb], yT3[:, f, :], ow16v[:, f, nb * 512:(nb + 1) * 512], f == 0, f == 15),
                             reads=[RyT, Row], writes=[PBR[pb]])
                    S.op("dve", TT(ptmp, PB[pb], gbc[:, nb * 512:(nb + 1) * 512], ALU.mult), reads=[PBR[pb], Rg], writes=[Rptmp])
                    S.op("dve", TT(hnew[:, nb * 512:(nb + 1) * 512], ptmp, hold[:, nb * 512:(nb + 1) * 512], ALU.add),
                         reads=[Rptmp, Rhold], writes=[Rhnew])
                if not last:
                    S.dma("pool", h_d[c * 128:(c + 1) * 128, :], hnew, reads=[Rhnew], writes=[Rh_d[c]])
                else:
                    S.op("act", ACT(ttmp[:, 0:512], hnew[:, 0:512], AF.Square, accum=fss), reads=[Rhnew], writes=[Rttmp, Rfss])
                    S.op("act", ACT(ttmp[:, 0:512], hnew[:, 512:1024], AF.Square, accum=frs), reads=[Rhnew], writes=[Rttmp, Rfrs])
                    S.op("dve", TT(fss, fss, frs, ALU.add), reads=[Rfss, Rfrs], writes=[Rfss])
                    S.op("act", ACT(frs, fss, AF.Ln, scale=1.0 / D, bias=eps_c), reads=[Rfss, Rc], writes=[Rfrs])
                    S.op("act", ACT(frs, frs, AF.Exp, scale=-0.5), reads=[Rfrs], writes=[Rfrs])
                    S.op("dve", STT(hold, hnew, frs, fnwbc, ALU.mult, ALU.mult), reads=[Rhnew, Rfrs, Rfnw], writes=[Rhold])
                    final_ops.append(S.dma("pool", out_d[(c - 2) * 128:(c - 1) * 128, :], hold, reads=[Rhold]))

            issue_loads(0)
            for g, (dirn, c) in enumerate(steps):
                if g + 1 < len(steps):
                    issue_loads(g + 1)
                ls = g if dirn == 0 else g - NT
                if ls == 0:
                    allS = [RS32h[h][d] for h in range(4) for d in range(2)]
                    S.op("pool", MSET(S32, 0.0), writes=[RS32] + allS)
                    for b in range(2):
                        S.op("pool", MSET(S16[b][0], 0.0), writes=[RS16[b][h][d] for h in range(4) for d in range(2)])
                mask = maskF if dirn == 0 else maskB
                bi = g % 2
                cur, nxt = ls % 2, (ls + 1) % 2
                qb, Rq = qc[bi]
                kb, Rk = kc[bi]
                ktb, Rkt = ktc[bi]
                vb, Rv = vc[bi]
                kub = ku[bi][0]
                sTb = sT16[bi][0]
                ob = osb[bi][0]
                qb3 = qb.rearrange("p (a t) -> p a t", t=128)
                kb3 = kb.rearrange("p (a t) -> p a t", t=128)
                for h in range(4):
                    sb_ = h % 2
                    sps = PB[sb_][:, 0:128]
                    for d2 in range(ndt):
                        S.op("pe", MM(sps, kb3[:, h * ndt + d2, :], qb3[:, h * ndt + d2, :], d2 == 0, d2 == ndt - 1),
                             reads=[Rk, Rq], writes=[PBR[sb_]])
                    sTh = sTb[:, h * 128:(h + 1) * 128]
                    if scalar_mix:
                        S.op("dve", STT(sTh, sps, scol(wsc, c, dirn, h), mask[:], ALU.mult, ALU.mult),
                             reads=[PBR[sb_], Rwsc, Rc], writes=[RsTh[bi][h]])
                    else:
                        S.op("dve", TT(sTh, sps, mask[:], ALU.mult), reads=[PBR[sb_], Rc], writes=[RsTh[bi][h]])
                if kind == "mlstm":
                    for h in range(4):
                        sTh = sTb[:, h * 128:(h + 1) * 128]
                        dps = PB[6][:, h:h + 1]
                        S.op("pe", MM(dps, sTh, ones16[:, 0:1], True, False), reads=[RsTh[bi][h], Rc], writes=[PBR[6]])
                        for d2 in range(ndt):
                            S.op("pe", MM(dps, qb3[:, h * ndt + d2, :], S16v[cur][:, h, d2, 512:513], False, d2 == ndt - 1),
                                 reads=[Rq, RS16[cur][h][d2]], writes=[PBR[6]])
                    o8 = (c * 8 + dirn * 4)
                    S.op("act", ACT(dna, PB[6][:, 0:4], AF.Abs), reads=[PBR[6]], writes=[Rdna])
                    S.op("dve", TT(dna, dna, emB[:, o8:o8 + 4], ALU.max), reads=[Rdna, RemB], writes=[Rdna])
                    S.op("dve", lambda e: e.reciprocal(out=sc2, in_=dna), reads=[Rdna], writes=[Rsc2])
                if scalar_mix:
                    for h in range(4):
                        S.op("dve", TS(kub[:, h * dk:(h + 1) * dk], ktb[:, h * dk:(h + 1) * dk], scol(usc, c, dirn, h), ALU.mult),
                             reads=[Rkt, Rusc], writes=[Rkuh[bi][h]])
                p5_B1(g - 2)
                for h in range(4):
                    if scalar_mix:
                        ksrc_tm, Rks = kub, Rkuh[bi][h]
                    else:
                        ksrc_tm, Rks = ktb, Rkt
                    for d2 in range(ndt):
                        dsp = 4 + (h * ndt + d2) % 2
                        lw = ksrc_tm[:, h * dk + d2 * 128: h * dk + (d2 + 1) * 128]
                        S.op("pe", MM(PB[dsp], lw, vb[:, h * 512:(h + 1) * 512], True, True), reads=[Rks, Rv], writes=[PBR[dsp]])
                        dcol = scol(dec, c, dirn, h) if scalar_mix else gdec4[:, dirn, h, c:c + 1]
                        S.op("dve", STT(S32v[:, h, d2, 0:512], S32v[:, h, d2, 0:512], dcol, PB[dsp], ALU.mult, ALU.add),
                             reads=[PBR[dsp], RS32h[h][d2], Rdec], writes=[RS32h[h][d2]])
                    if kind == "mlstm":
                        for d2 in range(ndt):
                            lw = kub[:, h * dk + d2 * 128: h * dk + (d2 + 1) * 128]
                            S.op("pe", MM(PB[7][:, 8 + h * 2 + d2:9 + h * 2 + d2], lw, ones16[:, 0:1], True, True),
                                 reads=[Rkuh[bi][h], Rc], writes=[PBR[7]])
                        for d2 in range(ndt):
                            S.op("dve", STT(S32v[:, h, d2, 512:513], S32v[:, h, d2, 512:513], scol(dec, c, dirn, h),
                                            PB[7][:, 8 + h * 2 + d2:9 + h * 2 + d2], ALU.mult, ALU.add),
                                 reads=[PBR[7], RS32h[h][d2], Rdec], writes=[RS32h[h][d2]])
                    sTh = sTb[:, h * 128:(h + 1) * 128]
                    ops_ = 2 + h % 2
                    S.op("pe", MM(PB[ops_], sTh, vb[:, h * 512:(h + 1) * 512], True, False), reads=[RsTh[bi][h], Rv], writes=[PBR[ops_]])
                    for d2 in range(ndt):
                        S.op("pe", MM(PB[ops_], qb3[:, h * ndt + d2, :], S16v[cur][:, h, d2, 0:512], False, d2 == ndt - 1),
                             reads=[Rq, RS16[cur][h][d2]], writes=[PBR[ops_]])
                    oh = ob[:, h * 512:(h + 1) * 512]
                    if dirn == 0:
                        acc_ap, acc_res = sfw[:, c * 4 + h:c * 4 + h + 1], Rsfw[c]
                    else:
                        acc_ap, acc_res = sbw[bi][0][:, h:h + 1], sbw[bi][1]
                    if kind == "mlstm":
                        S.op("act", ACT(oh, PB[ops_], AF.Copy, scale=sc2[:, h:h + 1], accum=acc_ap), reads=[PBR[ops_], Rsc2], writes=[Rosbh[bi][h], acc_res])
                    elif kind == "ret":
                        S.op("act", ACT(oh, PB[ops_], AF.Copy, scale=scol(eB, c, dirn, h), accum=acc_ap), reads=[PBR[ops_], ReB], writes=[Rosbh[bi][h], acc_res])
                    else:
                        S.op("act", ACT(oh, PB[ops_], AF.Copy), reads=[PBR[ops_]], writes=[Rosbh[bi][h]])
                    for d2 in range(ndt):
                        if True:
                            S.op("act", ACT(S16v[nxt][:, h, d2, 0:513], S32v[:, h, d2, 0:513], AF.Copy),
                                 reads=[RS32h[h][d2]], writes=[RS16[nxt][h][d2]])
                        else:
                            S.op("pool", CP(S16v[nxt][:, h, d2, 0:513], S32v[:, h, d2, 0:513]),
                                 reads=[RS32h[h][d2]], writes=[RS16[nxt][h][d2]])
                    if h == 0:
                        p5_A1(g - 1)
                    if h == 1:
                        p5_A1act(g - 1)
                    if h == 2:
                        p5_mid(g - 1)
                if dirn == 0:
                    S.dma("act", ofw_d[c * 128:(c + 1) * 128, :], ob, reads=Rosbh[bi], writes=[Rofw_d[c]])
                p5_A2(g - 1)
                p5_B2(g - 2)
                issue_p5_loads(g + 1)
                issue_hold_load(g)
            LS = len(steps)
            p5_B1(LS - 2)
            p5_B2(LS - 2)
            p5_A1(LS - 1)
            p5_A1act(LS - 1)
            p5_mid(LS - 1)
            p5_A2(LS - 1)
            p5_B1(LS - 1)
            p5_B2(LS - 1)
            S.barrier()
        S.emit(final_ops=final_ops)
    return nc


def CP_ACT(out, in_):
    return lambda e: e.activation(out=out, in_=in_, func=AF.Copy)


def _rope_tables():
    inv_freq = (np.float32(10000.0) ** (-np.arange(64, dtype=np.float32) / np.float32(64))).astype(np.float32)
    p = np.arange(128)
    cos = np.zeros((128, 32, 2, 64), np.float32)
    sin = np.zeros((128, 32, 2, 64), np.float32)
    for k in range(32):
        row = (2 * k + p // 64).astype(np.float32)
        col = (p % 64).astype(np.float32)
        for s, pos in enumerate((row, col)):
            ang = (pos[:, None] * inv_freq[None, :]).astype(np.float32)
            cos[:, k, s, :] = np.cos(ang)
            sin[:, k, s, :] = np.sin(ang)
    return cos.reshape(128, 32 * 128), sin.reshape(128, 32 * 128)


_INPUT_NAMES = (
    "x", "c", "ctx", "c_ctx",
    "l0_norm_w", "l0_ada_w", "l0_ada_b", "l0_in_w", "l0_conv_w", "l0_conv_b", "l0_gate_b", "l0_head_norm_w", "l0_out_w",
    "l1_norm_w", "l1_ada_w", "l1_ada_b", "l1_in_w", "l1_gk_w2", "l1_gk_b", "l1_head_norm_w", "l1_out_w",
    "l2_norm_w", "l2_ada_w", "l2_ada_b", "l2_in_w", "l2_decay_logit", "l2_head_norm_w", "l2_out_w",
    "l3_norm_w", "l3_ada_w", "l3_ada_b", "l3_in_w", "l3_conv_w", "l3_conv_b", "l3_gate_b", "l3_head_norm_w", "l3_out_w",
    "final_norm_w",
)


def make_in_maps(inputs, ncores=8):
    missing = [n for n in _INPUT_NAMES if n not in inputs]
    assert not missing, missing
    f = lambda a: np.ascontiguousarray(np.asarray(a, dtype=np.float32))
    shared = {}
    idx = np.arange(128)
    shared["ident16"] = np.eye(128, dtype=np.float32).astype(ml_dtypes.bfloat16)
    shared["maskF"] = (idx[None, :] >= idx[:, None]).astype(np.float32)
    shared["maskB"] = (idx[None, :] <= idx[:, None]).astype(np.float32)
    cos, sin = _rope_tables()
    shared["rope_cos"] = cos
    shared["rope_sin"] = sin
    shared["final_norm_w"] = f(inputs["final_norm_w"])
    for li in range(4):
        kind = KINDS[li]
        p = "l%d_" % li
        for nm in ("norm_w", "ada_w", "ada_b", "in_w", "out_w"):
            shared[p + nm] = f(inputs[p + nm])
        shared[p + "hnwT"] = np.ascontiguousarray(f(inputs[p + "head_norm_w"]).reshape(16, 128).T)
        if kind == "mlstm":
            cw = f(inputs[p + "conv_w"]).reshape(9, 16, 128)
            shared[p + "convw"] = np.ascontiguousarray(cw.transpose(2, 1, 0).reshape(128, 16 * 9))
            shared[p + "convb"] = np.ascontiguousarray(f(inputs[p + "conv_b"]).reshape(16, 128).T)
            shared[p + "gate_b"] = f(inputs[p + "gate_b"])
        elif kind == "gla":
            shared[p + "w2"] = np.ascontiguousarray(f(inputs[p + "gk_w2"]).transpose(1, 0, 2).reshape(16, 1024))
            gb = f(inputs[p + "gk_b"]).reshape(2, 4, 128)
            shared[p + "gkb"] = np.ascontiguousarray(gb.transpose(2, 0, 1).reshape(128, 8))
        else:
            shared[p + "decay"] = f(inputs[p + "decay_logit"]).reshape(8)
    x = f(inputs["x"])
    ctx = f(inputs["ctx"])
    c = f(inputs["c"])
    c_ctx = f(inputs["c_ctx"])
    maps = []
    for b in range(ncores):
        m = dict(shared)
        m["x"] = x[b]
        m["ctx"] = ctx[b]
        cc = np.stack([c[b], c_ctx], axis=0)
        m["ccT"] = np.ascontiguousarray(cc.reshape(2, 8, 128).transpose(2, 1, 0).reshape(128, 16))
        maps.append(m)
    return maps


_NC_CACHE = {}


def kernel(**inputs):
    if 4 not in _NC_CACHE:
        _NC_CACHE[4] = build(4)
    nc = _NC_CACHE[4]
    maps = make_in_maps(inputs, 8)
    res = run_bass_kernel_spmd(nc, maps, core_ids=list(range(8)))
    return np.stack([np.asarray(r["out"], dtype=np.float32) for r in res.results], axis=0)
```
